# Optimizing a Trainium2 kernel written in Bass

```python
import jax, jax.numpy as jnp
from jax import lax
import numpy as np

D_MODEL = 2048
BATCH = 8
SEQ = 4096
DEPTH = 4

N_MEM = 256
N_A_LAYERS = DEPTH // 2
N_B_LAYERS = DEPTH - N_A_LAYERS
GLA_HEADS = 4
GLA_DK = D_MODEL // 8
GLA_DV = 3 * D_MODEL // 16
GLA_QK_WIDTH = GLA_HEADS * GLA_DK
GLA_V_WIDTH = GLA_HEADS * GLA_DV
GLA_GATE_RANK = 16
GLA_GATE_TAU = 16.0
GLA_CHUNK = 64
FOX_HEADS = 12
FOX_HEAD_DIM = D_MODEL // 16
FOX_WIDTH = FOX_HEADS * FOX_HEAD_DIM
FOX_BLOCK = 128
MEM_HEADS = 4
MEM_HEAD_DIM = D_MODEL // 16
MEM_WIDTH = MEM_HEADS * MEM_HEAD_DIM
A_MIX_WIDTH = GLA_V_WIDTH + MEM_WIDTH
B_MIX_WIDTH = FOX_WIDTH + MEM_WIDTH
A_IN_WIDTH = 2 * GLA_QK_WIDTH + GLA_V_WIDTH + GLA_GATE_RANK + GLA_V_WIDTH + MEM_WIDTH
B_IN_WIDTH = FOX_WIDTH + MEM_WIDTH
FFN_HIDDEN = 11 * D_MODEL // 4
CONV_WIDTH = 3
EPS = 1e-6

kernel_name = 'hybrid_gla_fox_yoco_block'


def rmsnorm(x, g):
    xf = x.astype(jnp.float32)
    y = xf * lax.rsqrt(jnp.mean(xf * xf, axis=-1, keepdims=True) + EPS)
    return (y * g.astype(jnp.float32)).astype(x.dtype)


def split_cols(z, widths):
    out, start = [], 0
    for w in widths:
        out.append(z[..., start:start + w])
        start += w
    return out


def split_heads(t, n_heads):
    b, s, _ = t.shape
    return t.reshape(b, s, n_heads, -1).transpose(0, 2, 1, 3)


def merge_heads(t):
    b, h, s, d = t.shape
    return t.transpose(0, 2, 1, 3).reshape(b, s, h * d)


def memory_kv(mem_n, w_kv):
    mk, mv = split_cols(mem_n @ w_kv, [MEM_WIDTH, MEM_WIDTH])
    return split_heads(mk, MEM_HEADS), split_heads(mv, MEM_HEADS)


def memory_attention(q, mem_k, mem_v):
    qh = split_heads(q, MEM_HEADS)
    s = jnp.einsum('bhqd,bhkd->bhqk', qh, mem_k).astype(jnp.float32) * (MEM_HEAD_DIM ** -0.5)
    p = jax.nn.softmax(s, axis=-1).astype(mem_v.dtype)
    return merge_heads(jnp.einsum('bhqk,bhkd->bhqd', p, mem_v))


def gla_chunked(q, k, v, g):
    q, k, v, g = (t.astype(jnp.float32) for t in (q, k, v, g))
    b_, h_, s_, dk = q.shape
    dv = v.shape[-1]
    nc = s_ // GLA_CHUNK

    def to_chunks(t):
        return jnp.moveaxis(t.reshape(b_, h_, nc, GLA_CHUNK, t.shape[-1]), 2, 0)

    tri = jnp.tril(jnp.ones((GLA_CHUNK, GLA_CHUNK), dtype=bool))[:, :, None]

    def step(state, inp):
        qc, kc, vc, gc = inp
        bcum = jnp.cumsum(gc, axis=2)
        o_inter = jnp.einsum('bhtk,bhkv->bhtv', qc * jnp.exp(bcum), state)
        rel = bcum[:, :, :, None, :] - bcum[:, :, None, :, :]
        decay = jnp.exp(jnp.where(tri, rel, -jnp.inf))
        att = jnp.einsum('bhtk,bhsk,bhtsk->bhts', qc, kc, decay)
        o_intra = jnp.einsum('bhts,bhsv->bhtv', att, vc)
        b_last = bcum[:, :, -1:, :]
        new_state = jnp.exp(b_last)[:, :, 0, :, None] * state + jnp.einsum(
            'bhsk,bhsv->bhkv', kc * jnp.exp(b_last - bcum), vc)
        return new_state, o_inter + o_intra

    state0 = jnp.zeros((b_, h_, dk, dv), jnp.float32)
    _, out = lax.scan(step, state0, (to_chunks(q), to_chunks(k), to_chunks(v), to_chunks(g)))
    return jnp.moveaxis(out, 0, 2).reshape(b_, h_, s_, dv)


def gla_mixer(h, mem_k, mem_v, w_in, w_gate_up, b_gate, gn_gain, w_out):
    z = h @ w_in
    q, k, v, glr, og, mq = split_cols(
        z, [GLA_QK_WIDTH, GLA_QK_WIDTH, GLA_V_WIDTH, GLA_GATE_RANK, GLA_V_WIDTH, MEM_WIDTH])
    g = jax.nn.log_sigmoid((glr @ w_gate_up + b_gate).astype(jnp.float32)) / GLA_GATE_TAU
    o = gla_chunked(split_heads(q, GLA_HEADS) * (GLA_DK ** -0.5), split_heads(k, GLA_HEADS),
                    split_heads(v, GLA_HEADS), split_heads(g, GLA_HEADS))
    o = o * lax.rsqrt(jnp.mean(o * o, axis=-1, keepdims=True) + EPS)
    o = o * gn_gain.astype(jnp.float32).reshape(GLA_HEADS, 1, GLA_DV)
    o = merge_heads(o).astype(h.dtype) * jax.nn.silu(og)
    m = memory_attention(mq, mem_k, mem_v)
    return jnp.concatenate([o, m], axis=-1) @ w_out


def fox_shared_kv(x, g_kv, w_kv, b_f):
    hs = rmsnorm(x, g_kv)
    k, v, fl = split_cols(hs @ w_kv, [FOX_WIDTH, FOX_WIDTH, FOX_HEADS])
    log_f = jax.nn.log_sigmoid((fl + b_f).astype(jnp.float32))
    c = jnp.cumsum(log_f, axis=1).transpose(0, 2, 1)
    return split_heads(k, FOX_HEADS), split_heads(v, FOX_HEADS), c


def fox_attention(q, k, v, c):
    b_, h_, s_, dh = q.shape
    nb = s_ // FOX_BLOCK
    qb = jnp.moveaxis(q.reshape(b_, h_, nb, FOX_BLOCK, dh), 2, 0)
    cb = jnp.moveaxis(c.reshape(b_, h_, nb, FOX_BLOCK), 2, 0)
    kpos = jnp.arange(s_)
    scale = FOX_HEAD_DIM ** -0.5

    def block(args):
        qi, ci, i = args
        s = jnp.einsum('bhqd,bhkd->bhqk', qi, k).astype(jnp.float32) * scale
        s = s + ci[..., None] - c[:, :, None, :]
        qpos = i * FOX_BLOCK + jnp.arange(FOX_BLOCK)
        s = jnp.where(kpos[None, :] <= qpos[:, None], s, -jnp.inf)
        p = jax.nn.softmax(s, axis=-1).astype(v.dtype)
        return jnp.einsum('bhqk,bhkd->bhqd', p, v)

    out = lax.map(block, (qb, cb, jnp.arange(nb)))
    return jnp.moveaxis(out, 0, 2).reshape(b_, h_, s_, dh)


def fox_mixer(h, fk, fv, fc, mem_k, mem_v, w_in, w_out):
    q, mq = split_cols(h @ w_in, [FOX_WIDTH, MEM_WIDTH])
    o = merge_heads(fox_attention(split_heads(q, FOX_HEADS), fk, fv, fc))
    m = memory_attention(mq, mem_k, mem_v)
    return jnp.concatenate([o, m], axis=-1) @ w_out


def causal_dwconv(u, w, b):
    s_ = u.shape[1]
    up = jnp.pad(u, ((0, 0), (CONV_WIDTH - 1, 0), (0, 0)))
    return w[0] * up[:, 0:s_] + w[1] * up[:, 1:s_ + 1] + w[2] * up[:, 2:s_ + 2] + b


def conv_ffn(h, w_up, conv_w, conv_b, w_down):
    u = causal_dwconv(h @ w_up, conv_w, conv_b)
    a, val = split_cols(u, [FFN_HIDDEN, FFN_HIDDEN])
    return (jax.nn.silu(a) * val) @ w_down


def setup_inputs(seed: int = 0) -> dict:
    key = jax.random.key(seed)
    ks = jax.random.split(key, 24)

    def nrm(k, shape, scale):
        return jax.random.normal(k, shape, jnp.float32) * scale

    out_scale = (2.0 * DEPTH) ** -0.5
    return {
        'x': nrm(ks[0], (BATCH, SEQ, D_MODEL), 1.0),
        'mem': nrm(ks[1], (BATCH, N_MEM, D_MODEL), 1.0),
        'norm_mix': 1.0 + nrm(ks[2], (DEPTH, D_MODEL), 0.02),
        'norm_ffn': 1.0 + nrm(ks[3], (DEPTH, D_MODEL), 0.02),
        'norm_mem': 1.0 + nrm(ks[4], (D_MODEL,), 0.02),
        'norm_final': 1.0 + nrm(ks[5], (D_MODEL,), 0.02),
        'mem_w_kv': nrm(ks[6], (DEPTH, D_MODEL, 2 * MEM_WIDTH), D_MODEL ** -0.5),
        'gla_w_in': nrm(ks[7], (N_A_LAYERS, D_MODEL, A_IN_WIDTH), D_MODEL ** -0.5),
        'gla_w_gate_up': nrm(ks[8], (N_A_LAYERS, GLA_GATE_RANK, GLA_QK_WIDTH), GLA_GATE_RANK ** -0.5),
        'gla_b_gate': 1.0 + nrm(ks[9], (N_A_LAYERS, GLA_QK_WIDTH), 0.5),
        'gla_norm': 1.0 + nrm(ks[10], (N_A_LAYERS, GLA_V_WIDTH), 0.02),
        'gla_w_out': nrm(ks[11], (N_A_LAYERS, A_MIX_WIDTH, D_MODEL), A_MIX_WIDTH ** -0.5 * out_scale),
        'fox_kv_norm': 1.0 + nrm(ks[12], (D_MODEL,), 0.02),
        'fox_w_kv': nrm(ks[13], (D_MODEL, 2 * FOX_WIDTH + FOX_HEADS), D_MODEL ** -0.5),
        'fox_b_f': 3.0 + nrm(ks[14], (FOX_HEADS,), 1.0),
        'fox_w_in': nrm(ks[15], (N_B_LAYERS, D_MODEL, B_IN_WIDTH), D_MODEL ** -0.5),
        'fox_w_out': nrm(ks[16], (N_B_LAYERS, B_MIX_WIDTH, D_MODEL), B_MIX_WIDTH ** -0.5 * out_scale),
        'ffn_w_up': nrm(ks[17], (DEPTH, D_MODEL, 2 * FFN_HIDDEN), D_MODEL ** -0.5),
        'ffn_conv_w': nrm(ks[18], (DEPTH, CONV_WIDTH, 2 * FFN_HIDDEN), CONV_WIDTH ** -0.5),
        'ffn_conv_b': nrm(ks[19], (DEPTH, 2 * FFN_HIDDEN), 0.01),
        'ffn_w_down': nrm(ks[20], (DEPTH, FFN_HIDDEN, D_MODEL), FFN_HIDDEN ** -0.5 * out_scale),
    }


def reference(x, mem, norm_mix, norm_ffn, norm_mem, norm_final, mem_w_kv, gla_w_in, gla_w_gate_up,
              gla_b_gate, gla_norm, gla_w_out, fox_kv_norm, fox_w_kv, fox_b_f, fox_w_in, fox_w_out,
              ffn_w_up, ffn_conv_w, ffn_conv_b, ffn_w_down):
    mem_n = rmsnorm(mem, norm_mem)
    fk = fv = fc = None
    for i in range(DEPTH):
        if i == N_A_LAYERS:
            fk, fv, fc = fox_shared_kv(x, fox_kv_norm, fox_w_kv, fox_b_f)
        mk, mv = memory_kv(mem_n, mem_w_kv[i])
        h = rmsnorm(x, norm_mix[i])
        if i < N_A_LAYERS:
            x = x + gla_mixer(h, mk, mv, gla_w_in[i], gla_w_gate_up[i], gla_b_gate[i],
                              gla_norm[i], gla_w_out[i])
        else:
            j = i - N_A_LAYERS
            x = x + fox_mixer(h, fk, fv, fc, mk, mv, fox_w_in[j], fox_w_out[j])
        h = rmsnorm(x, norm_ffn[i])
        x = x + conv_ffn(h, ffn_w_up[i], ffn_conv_w[i], ffn_conv_b[i], ffn_w_down[i])
    return rmsnorm(x, norm_final)
```

```python
import numpy as np
import concourse.bass as bass
import concourse.mybir as mybir
from concourse.bass_utils import run_bass_kernel_spmd

F32 = mybir.dt.float32
BF16 = mybir.dt.bfloat16
AF = mybir.ActivationFunctionType
ALU = mybir.AluOpType
AX = mybir.AxisListType

S = 4096
D = 2048
KC = D // 128
TT = 512
NT = S // TT
NMEM = 256
FFN_H = 5632
EPS = 1e-6


def _dsize(dt):
    return 4 if dt == F32 else 2


class Op:
    __slots__ = ("eng", "fn", "deps", "dma", "sig", "tok", "waits", "slot", "q", "nobar")


class Prog:
    COMPUTE = ("pe", "act", "dve", "pool")
    ALL = ("pe", "act", "dve", "pool", "sp")
    QUEUES = {"sp": ("sp", 14), "pool": ("pool", 12), "poolc": ("pool", 4), "act": ("act", 4)}

    def __init__(self, nc):
        self.nc = nc
        self.ops = []
        self.lw = {}
        self.rd = {}
        self.last = {e: None for e in self.ALL}
        self.dmas = {q: [] for q in self.QUEUES}
        self.persist = {}

    def add(self, eng, fn, r=(), w=(), dma=False, nobar=False):
        op = Op()
        op.q = eng
        op.nobar = nobar
        if dma:
            eng = self.QUEUES[op.q][0]
        op.eng, op.fn, op.dma, op.sig, op.tok, op.slot = eng, fn, dma, False, None, None
        deps = []
        for k in r:
            o = self.lw.get(k)
            if o is not None:
                deps.append(o)
        for k in w:
            o = self.lw.get(k)
            if o is not None:
                deps.append(o)
            rk = self.rd.get(k)
            if rk:
                deps.extend(rk[0].values())
                deps.extend(rk[1])
        if dma:
            q = self.dmas[op.q]
            ns = self.QUEUES[op.q][1]
            if len(q) >= ns:
                deps.append(q[-ns])
            q.append(op)
        op.deps = deps
        for k in r:
            rk = self.rd.get(k)
            if rk is None:
                rk = self.rd[k] = ({}, [])
            if dma:
                rk[1].append(op)
            else:
                rk[0][eng] = op
        for k in w:
            self.lw[k] = op
            self.rd[k] = ({}, [])
            if nobar:
                self.persist[k] = op
        if not dma:
            self.last[eng] = op
        self.ops.append(op)
        return op

    def barrier(self):
        deps = [o for o in self.last.values() if o is not None]
        for qn, (e, ns) in self.QUEUES.items():
            deps.extend([o for o in self.dmas[qn] if not o.nobar][-ns:])
        for e in self.ALL:
            op = Op()
            op.eng, op.fn, op.dma, op.sig, op.tok, op.slot, op.q, op.nobar = e, None, False, False, None, None, e, False
            op.deps = list(deps)
            self.ops.append(op)
        self.lw = dict(self.persist)
        self.rd.clear()

    def emit(self, stack):
        nc = self.nc
        sems = {e: stack.enter_context(nc.semaphore("s_" + e)) for e in self.COMPUTE}
        dsem = {qn: [stack.enter_context(nc.semaphore("d_%s%d" % (qn, i))) for i in range(ns)]
                for qn, (e, ns) in self.QUEUES.items()}
        for op in self.ops:
            for d in op.deps:
                if d.dma:
                    continue
                if d.eng == op.eng and op.eng == "pe" and not op.dma:
                    continue
                d.sig = True
        cnt = {e: 0 for e in self.COMPUTE}
        dcnt = {qn: 0 for qn in self.QUEUES}
        for op in self.ops:
            if op.dma:
                i = dcnt[op.q]
                dcnt[op.q] += 1
                ns = self.QUEUES[op.q][1]
                op.slot = dsem[op.q][i % ns]
                op.tok = 16 * (i // ns + 1)
            elif op.sig:
                cnt[op.eng] += 1
                op.tok = cnt[op.eng]
        seen = {e: {} for e in self.ALL}
        per = {e: [] for e in self.ALL}
        for op in self.ops:
            sn = seen[op.eng]
            waits = {}
            for d in op.deps:
                if d.dma:
                    sem, val = d.slot, d.tok
                else:
                    if d.eng == op.eng and op.eng == "pe" and not op.dma:
                        continue
                    sem, val = sems[d.eng], d.tok
                key = id(sem)
                if sn.get(key, 0) >= val:
                    continue
                if key not in waits or waits[key][1] < val:
                    waits[key] = (sem, val)
            for key, (sem, val) in waits.items():
                sn[key] = val
            op.waits = list(waits.values())
            per[op.eng].append(op)
        self.n_ops = {e: len(per[e]) for e in self.ALL}

        def run(ename, e):
            for op in per[ename]:
                for sem, val in op.waits:
                    e.wait_ge(sem, val)
                if op.fn is None:
                    continue
                ins = op.fn(e)
                if op.dma:
                    ins.then_inc(op.slot, 16)
                elif op.sig:
                    ins.then_inc(sems[ename], 1)

        block = stack.enter_context(nc.Block())

        @block.tensor
        def _(e):
            run("pe", e)

        @block.scalar
        def _(e):
            run("act", e)

        @block.vector
        def _(e):
            run("dve", e)

        @block.gpsimd
        def _(e):
            run("pool", e)

        @block.sync
        def _(e):
            run("sp", e)


class Arena:
    def __init__(self, ap, nwords):
        self.ap = ap
        self.n = nwords
        self.off = 0

    def reset(self, off=0):
        self.off = off

    def tile(self, free, dt=F32, parts=128):
        if isinstance(free, int):
            free = (free,)
        n = 1
        for f in free:
            n *= f
        words = (n * _dsize(dt) + 3) // 4
        words = (words + 7) // 8 * 8
        assert self.off + words <= self.n, ("SBUF arena overflow", self.off, words, self.n)
        a = self.ap[0:parts, self.off:self.off + words]
        self.off += words
        if dt != F32:
            a = a.bitcast(dt)
        a = a[:, 0:n]
        if len(free) == 2:
            a = a.rearrange("p (a b) -> p a b", b=free[1])
        elif len(free) == 3:
            a = a.rearrange("p (a b c) -> p a b c", b=free[1], c=free[2])
        return a


COLS = {}
_off = 0
for _name, _n in (("nmix", 64), ("nffn", 64), ("nmem", 16), ("nfin", 16), ("nfkv", 16), ("bgate", 16),
                  ("gnorm", 24), ("bf", 1), ("convw", 4 * 3 * 88), ("convb", 4 * 88)):
    COLS[_name] = _off
    _off += _n
NCOLS = (_off + 7) // 8 * 8
C_ID, C_TRI, C_NEG, C_SCAN = 0, 128, 192, 320
NCONST = 320 + 512


def _colify(v):
    v = np.asarray(v, np.float32)
    lead = int(np.prod(v.shape[:-1])) if v.ndim > 1 else 1
    n = v.shape[-1] // 128
    return np.ascontiguousarray(v.reshape(lead, n, 128).transpose(2, 0, 1).reshape(128, lead * n))


def pack_cols(inp):
    c = np.zeros((128, NCOLS), np.float32)

    def put(name, arr):
        c[:, COLS[name]:COLS[name] + arr.shape[1]] = arr
    put("nmix", _colify(inp["norm_mix"]))
    put("nffn", _colify(inp["norm_ffn"]))
    put("nmem", _colify(inp["norm_mem"]))
    put("nfin", _colify(inp["norm_final"]))
    put("nfkv", _colify(inp["fox_kv_norm"]))
    put("bgate", _colify(inp["gla_b_gate"]))
    put("gnorm", _colify(inp["gla_norm"]))
    c[0:12, COLS["bf"]] = np.asarray(inp["fox_b_f"], np.float32)
    put("convw", _colify(inp["ffn_conv_w"]))
    put("convb", _colify(inp["ffn_conv_b"]))
    return c


def make_consts():
    k = np.zeros((128, NCONST), np.float32)
    k[:, C_ID:C_ID + 128] = np.eye(128, dtype=np.float32)
    s = np.arange(128)[:, None]
    t = np.arange(128)[None, :]
    k[0:64, C_TRI:C_TRI + 64] = (s[0:64] <= t[:, 0:64]).astype(np.float32)
    k[:, C_NEG:C_NEG + 128] = np.where(s <= t, 0.0, -30000.0).astype(np.float32)
    m = np.ones(512, np.float32)
    m[::64] = 0.0
    k[:, C_SCAN:C_SCAN + 512] = m[None, :]
    return k


WEIGHTS = (("mem_w_kv", (4, 2048, 1024)), ("gla_w_in", (2, 2048, 5648)), ("gla_w_out", (2, 2048, 2048)),
           ("fox_w_kv", (2048, 3084)), ("fox_w_in", (2, 2048, 2048)), ("fox_w_out", (2, 2048, 2048)),
           ("ffn_w_up", (4, 2048, 11264)), ("ffn_w_down", (4, 5632, 2048)))


def id_of(ap):
    return (ap.name, int(ap.offset))


class Builder:
    NWP = 4

    def __init__(self, debug_outs=()):
        nc = bass.Bass("TRN2", target_bir_lowering=False)
        self.nc = nc

        def di(name, shape, dt=F32):
            return nc.dram_tensor(name, list(shape), dt, kind="ExternalInput").ap()

        def scr(name, shape, dt):
            return nc.dram_tensor(name, list(shape), dt, kind="Internal").ap()
        self.xT_in = di("xT", [D, S])
        self.memT = di("memT", [D, NMEM])
        self.cols_d = di("cols", [128, NCOLS])
        self.consts_d = di("consts", [128, NCONST])
        self.wg_d = di("gla_w_gate_up", [2, 16, 1024])
        self.w32 = {n: di(n, s) for n, s in WEIGHTS}
        self.outT = nc.dram_tensor("outT", [D, S], F32, kind="ExternalOutput").ap()
        self.wb = {n: scr(n + "_bf", s, BF16) for n, s in WEIGHTS}
        self.XT = scr("XT", [D, S], F32)
        self.QT = scr("QT", [1024, S], F32)
        self.KT = scr("KT", [1024, S], F32)
        self.GT = scr("GT", [1024, S], F32)
        self.OGT = scr("OGT", [1536, S], F32)
        self.VTM = scr("VTM", [S, 1536], BF16)
        self.MQT = scr("MQT", [512, S], BF16)
        self.MIXT = scr("MIXT", [D, S], BF16)
        self.FQT = scr("FQT", [1536, S], BF16)
        self.FKT = scr("FKT", [1536, S], BF16)
        self.FVTM = scr("FVTM", [S, 1536], BF16)
        self.AUGQ = scr("AUGQ", [12, 6, S], BF16)
        self.AUGK = scr("AUGK", [12, 6, S], BF16)
        self.scratch = dict(XT=self.XT, QT=self.QT, KT=self.KT, GT=self.GT, OGT=self.OGT, VTM=self.VTM,
                            MQT=self.MQT, MIXT=self.MIXT, FQT=self.FQT, FKT=self.FKT, FVTM=self.FVTM,
                            AUGQ=self.AUGQ, AUGK=self.AUGK)
        self.debug_outs = {}
        for n in debug_outs:
            a = self.scratch[n]
            self.debug_outs[n] = nc.dram_tensor("dbg_" + n, list(a.shape), a.dtype, kind="ExternalOutput").ap()
        NW = 52000
        self.arena_t = nc.alloc_sbuf_tensor("arena", [128, NW], F32)
        self.A = Arena(self.arena_t.ap(), NW)
        self.banks = [nc.alloc_psum_tensor("ps%d" % i, [128, 512], F32).ap() for i in range(8)]
        self.ps_cfg = {"main": list(range(8))}
        self.ps_cnt = {}
        self.P = Prog(nc)
        self.pools = {}
        self.evi = 0
        self.pump_n = 3
        self.conv_init()
        self._persistent()

    def ps(self, role="main"):
        lst = self.ps_cfg[role]
        i = lst[self.ps_cnt.get(role, 0) % len(lst)]
        self.ps_cnt[role] = self.ps_cnt.get(role, 0) + 1
        return self.banks[i], ("ps", i)

    def mkpool(self, name, n, free, dt=F32):
        self.pools[name] = [[self.A.tile(free, dt) for _ in range(n)], 0]

    def rot(self, name):
        p = self.pools[name]
        i = p[1] % len(p[0])
        p[1] += 1
        return p[0][i], (name, i)

    def col(self, name, idx):
        o = COLS[name] + idx
        return self.cols[:, o:o + 1]

    def evac(self, out, in_, r, w, scale=None, eng=None):
        P = self.P
        if eng is None:
            eng = ("act", "dve")[self.evi % 2]
            self.evi += 1
        if eng == "act":
            if scale is None:
                P.add("act", lambda e: e.activation(out=out, in_=in_, func=AF.Copy), r=r, w=w)
            else:
                P.add("act", lambda e: e.activation(out=out, in_=in_, func=AF.Copy, scale=float(scale)), r=r, w=w)
        else:
            if scale is None:
                P.add(eng, lambda e: e.tensor_copy(out=out, in_=in_), r=r, w=w)
            else:
                P.add(eng, lambda e: e.tensor_scalar(out=out, in0=in_, scalar1=float(scale), scalar2=None,
                                                     op0=ALU.mult), r=r, w=w)

    def store(self, dst, src, r, q="pool"):
        self.P.add(q, lambda e: e.dma_start(out=dst, in_=src), r=r, dma=True)

    def load(self, dst, src, w, q="sp"):
        self.P.add(q, lambda e: e.dma_start(out=dst, in_=src), w=w, dma=True)

    def _persistent(self):
        A, P = self.A, self.P
        self.cols = A.tile(NCOLS)
        self.identb = A.tile(128, BF16)
        self.ones32 = A.tile(128)
        self.onesb = A.tile(128, BF16)
        self.trib = A.tile(64, BF16)
        self.negb = A.tile(128, BF16)
        self.scanmask = A.tile(512)
        self.ones512 = A.tile(512)
        self.epst = A.tile(1)
        self.negbg = A.tile(16)
        self.negbf = A.tile(1)
        self.memnT = A.tile((KC, NMEM), BF16)
        self.mkT = A.tile((4, NMEM), BF16)
        self.mv = A.tile((2, 512), BF16)
        self.wg = A.tile(1024, BF16)
        self.base = A.off
        cst = A.tile(NCONST)
        self.load(self.cols, self.cols_d, ["cols"])
        self.load(cst, self.consts_d, ["cst"])
        P.add("pool", lambda e: e.memset(self.ones32, 1.0), w=["ones32"])
        P.add("pool", lambda e: e.memset(self.onesb, 1.0), w=["onesb"])
        P.add("pool", lambda e: e.memset(self.ones512, 1.0), w=["ones512"])
        P.add("pool", lambda e: e.memset(self.epst, EPS), w=["epst"])
        P.add("dve", lambda e: e.tensor_copy(out=self.identb, in_=cst[:, C_ID:C_ID + 128]), r=["cst"], w=["identb"])
        P.add("dve", lambda e: e.tensor_copy(out=self.trib[0:64, :], in_=cst[0:64, C_TRI:C_TRI + 64]), r=["cst"], w=["trib"])
        P.add("dve", lambda e: e.tensor_copy(out=self.negb, in_=cst[:, C_NEG:C_NEG + 128]), r=["cst"], w=["negb"])
        P.add("dve", lambda e: e.tensor_copy(out=self.scanmask, in_=cst[:, C_SCAN:C_SCAN + 512]), r=["cst"], w=["scanmask"])
        ob = COLS["bgate"]
        P.add("dve", lambda e: e.tensor_scalar(out=self.negbg, in0=self.cols[:, ob:ob + 16], scalar1=-1.0, scalar2=None,
                                               op0=ALU.mult), r=["cols"], w=["negbg"])
        of = COLS["bf"]
        P.add("dve", lambda e: e.tensor_scalar(out=self.negbf, in0=self.cols[:, of:of + 1], scalar1=-1.0, scalar2=None,
                                               op0=ALU.mult), r=["cols"], w=["negbf"])
        P.barrier()

    def conv_init(self):
        order = [("mem_w_kv", i) for i in range(4)] + [("gla_w_in", 0), ("gla_w_out", 0), ("ffn_w_up", 0), ("ffn_w_down", 0),
                 ("gla_w_in", 1), ("gla_w_out", 1), ("ffn_w_up", 1), ("ffn_w_down", 1), ("fox_w_kv", None), ("fox_w_in", 0),
                 ("fox_w_out", 0), ("ffn_w_up", 2), ("ffn_w_down", 2), ("fox_w_in", 1), ("fox_w_out", 1), ("ffn_w_up", 3),
                 ("ffn_w_down", 3)]
        self.conv_q = []
        self.wkeys = {}
        self.conv_done = {}
        for n, li in order:
            src = self.w32[n] if li is None else self.w32[n][li]
            dst = self.wb[n] if li is None else self.wb[n][li]
            rows, colsn = src.shape
            step = max(1, (1 << 20) // colsn)
            keys = []
            for r0 in range(0, rows, step):
                r1 = min(rows, r0 + step)
                key = ("W", n, li, r0)
                keys.append(key)
                self.conv_q.append((key, src[r0:r1, :], dst[r0:r1, :]))
            self.wkeys[id_of(dst)] = keys
            self.conv_done[(n, li)] = len(self.conv_q)
        self.conv_i = 0

    def pump(self, n):
        while n > 0 and self.conv_i < len(self.conv_q):
            key, s_, d_ = self.conv_q[self.conv_i]
            self.conv_i += 1
            n -= 1
            self.P.add("poolc", lambda e, s_=s_, d_=d_: e.dma_start(out=d_, in_=s_), w=[key], dma=True, nobar=True)

    def ensure(self, n, li):
        self.pump(self.conv_done[(n, li)] - self.conv_i)

    def wload(self, w2d, k0, nk, c0, ncols):
        slot = self.wp_i % self.NWP
        self.wp_i += 1
        t = self.wpool[slot]
        key = ("wp", slot)
        src = w2d[k0 * 128:(k0 + nk) * 128, c0:c0 + ncols].rearrange("(kc p) f -> p kc f", p=128)
        self.P.add("sp", lambda e, d=t[:, 0:nk, 0:ncols]: e.dma_start(out=d, in_=src), r=self.wkeys[id_of(w2d)], w=[key], dma=True)
        return t, key

    def mk_wpool(self, n=4):
        self.NWP = n
        self.wpool = [self.A.tile((16, 512), BF16) for _ in range(n)]
        self.wp_i = 0

    def gemm_fm(self, At, akey, nkc, w2d, c0, ncols, consume, kblock=16, T=TT):
        P = self.P
        ci = 0
        for b0 in range(0, ncols, 512):
            nb = min(512, ncols - b0)
            nch = (nb + 127) // 128
            bks = [self.ps() for _ in range(nch)]
            for k0 in range(0, nkc, kblock):
                nk = min(kblock, nkc - k0)
                wt, wkey = self.wload(w2d, k0, nk, c0 + b0, nb)
                for ch in range(nch):
                    m = min(128, nb - ch * 128)
                    bank, bkey = bks[ch]
                    for k in range(nk):
                        kk = k0 + k
                        P.add("pe", lambda e, o=bank[0:m, 0:T], l=wt[:, k, ch * 128:ch * 128 + m], rr=At[:, kk, 0:T],
                              st=(kk == 0), sp=(kk == nkc - 1): e.matmul(o, l, rr, start=st, stop=sp),
                              r=[wkey, (akey, kk)], w=[bkey])
            for ch in range(nch):
                m = min(128, nb - ch * 128)
                consume(ci, bks[ch][0], bks[ch][1], m)
                ci += 1

    def gemm_tm(self, At, akey, nkc, w2d, c0, ncols, consume, T=TT):
        P = self.P
        for b0 in range(0, ncols, 512):
            nb = min(512, ncols - b0)
            wt, wkey = self.wload(w2d, 0, nkc, c0 + b0, nb)
            for tg in range(T // 128):
                bank, bkey = self.ps()
                for k in range(nkc):
                    P.add("pe", lambda e, o=bank[:, 0:nb], l=At[:, k, tg * 128:(tg + 1) * 128], rr=wt[:, k, 0:nb],
                          st=(k == 0), sp=(k == nkc - 1): e.matmul(o, l, rr, start=st, stop=sp),
                          r=[wkey, (akey, k)], w=[bkey])
                consume(tg, b0, nb, bank, bkey)

    def rstd_from(self, chunks, nfeat, out, okey, T=TT):
        P = self.P
        bank, bkey = self.ps()
        n = len(chunks)
        for i, (xa, xk) in enumerate(chunks):
            sq, sk = self.rot("sq")
            P.add("act", lambda e, o=sq[:, 0:T], a=xa: e.activation(out=o, in_=a, func=AF.Square),
                  r=(xk if isinstance(xk, list) else [xk]), w=[sk])
            P.add("pe", lambda e, o=bank[:, 0:T], rr=sq[:, 0:T], st=(i == 0), sp=(i == n - 1):
                  e.matmul(o, self.onesb, rr, start=st, stop=sp), r=[sk], w=[bkey])
        P.add("act", lambda e: e.activation(out=out, in_=bank[:, 0:T], func=AF.Sqrt, bias=self.epst, scale=1.0 / nfeat),
              r=[bkey], w=[okey])
        P.add("dve", lambda e: e.reciprocal(out=out, in_=out), r=[okey], w=[okey])

    def load_x(self, xt, src, t0, xk="x"):
        self.pump(self.pump_n)
        v = src[:, t0:t0 + TT].rearrange("(kc p) t -> p kc t", p=128)
        for k0 in range(0, KC, 4):
            self.load(xt[:, k0:k0 + 4, :], v[:, k0:k0 + 4, :], [(xk, k) for k in range(k0, k0 + 4)])

    def store_x(self, xt, dst, t0):
        v = dst[:, t0:t0 + TT].rearrange("(kc p) t -> p kc t", p=128)
        for k0 in range(0, KC, 4):
            self.store(v[:, k0:k0 + 4, :], xt[:, k0:k0 + 4, :], [("x", k) for k in range(k0, k0 + 4)])

    def norm_to_A(self, xt, At, gname, gidx0, rstd, xk="x", ak="A", rk="rstd"):
        P = self.P
        self.rstd_from([(xt[:, k, :], (xk, k)) for k in range(KC)], D, rstd, rk)
        for k in range(KC):
            P.add("dve", lambda e, o=At[:, k, :], a=xt[:, k, :], g=self.col(gname, gidx0 + k):
                  e.scalar_tensor_tensor(out=o, in0=a, scalar=g, in1=rstd, op0=ALU.mult, op1=ALU.mult),
                  r=[(xk, k), rk], w=[(ak, k)])

    def phase_mem_prep(self):
        A, P = self.A, self.P
        A.reset(self.base)
        self.ps_cfg = {"main": list(range(8))}
        mt = A.tile((KC, NMEM))
        rstd = A.tile(NMEM)
        self.mkpool("sq", 3, 512, BF16)
        v = self.memT.rearrange("(kc p) t -> p kc t", p=128)
        for k0 in range(0, KC, 4):
            self.load(mt[:, k0:k0 + 4, :], v[:, k0:k0 + 4, :], [("m", k) for k in range(k0, k0 + 4)])
        self.rstd_from([(mt[:, k, :], ("m", k)) for k in range(KC)], D, rstd, "mrstd", T=NMEM)
        for k in range(KC):
            P.add("dve", lambda e, o=self.memnT[:, k, :], a=mt[:, k, :], g=self.col("nmem", k):
                  e.scalar_tensor_tensor(out=o, in0=a, scalar=g, in1=rstd, op0=ALU.mult, op1=ALU.mult),
                  r=[("m", k), "mrstd"], w=[("memn", k)])
        P.barrier()

    def phase_mem_kv(self, li):
        A, P = self.A, self.P
        A.reset(self.base)
        self.ps_cfg = {"main": list(range(8))}
        w2d = self.wb["mem_w_kv"][li]
        wt = A.tile((KC, 1024), BF16)
        src = w2d.rearrange("(kc p) f -> p kc f", p=128)
        for k0 in range(0, KC, 4):
            self.P.add("sp", lambda e, d=wt[:, k0:k0 + 4, :], s_=src[:, k0:k0 + 4, :]: e.dma_start(out=d, in_=s_),
                       r=self.wkeys[id_of(w2d)], w=[("wkv", k0)], dma=True)
        for h in range(4):
            bank, bkey = self.ps()
            for k in range(KC):
                P.add("pe", lambda e, o=bank[:, 0:NMEM], l=wt[:, k, h * 128:(h + 1) * 128], rr=self.memnT[:, k, :],
                      st=(k == 0), sp=(k == KC - 1): e.matmul(o, l, rr, start=st, stop=sp),
                      r=[("wkv", k // 4 * 4)], w=[bkey])
            self.evac(self.mkT[:, h, :], bank[:, 0:NMEM], [bkey], [("mkT", h)])
        for jc in range(2):
            bank, bkey = self.ps()
            for k in range(KC):
                P.add("pe", lambda e, o=bank, l=self.memnT[:, k, jc * 128:(jc + 1) * 128], rr=wt[:, k, 512:1024],
                      st=(k == 0), sp=(k == KC - 1): e.matmul(o, l, rr, start=st, stop=sp),
                      r=[("wkv", k // 4 * 4)], w=[bkey])
            self.evac(self.mv[:, jc, :], bank, [bkey], [("mv", jc)])
        P.barrier()

    def _inproj_setup(self):
        A = self.A
        A.reset(self.base)
        self.ps_cfg = {"main": list(range(8))}
        xt1 = A.tile((KC, TT))
        self.xts = [xt1, xt1]
        self.Ats = [A.tile((KC, TT), BF16) for _ in range(2)]
        self.rstds = [A.tile(TT) for _ in range(2)]
        self.mk_wpool(3)
        self.mkpool("sq", 3, 512, BF16)
        self.mkpool("st32", 4, 512)
        self.mkpool("st16", 4, 512, BF16)

    def prep_x(self, xsrc, tt, gname, gidx0, load, norm):
        b = tt % 2
        if load:
            self.load_x(self.xts[b], xsrc, tt * TT, xk="x")
        if norm:
            self.norm_to_A(self.xts[b], self.Ats[b], gname, gidx0, self.rstds[b], xk="x", ak=("A", b), rk=("rstd", b))

    def fm_store(self, dst_rows, t0, dt, scale=None, func=None):
        P = self.P

        def consume(ci, bank, bkey, m):
            st, sk = self.rot("st32" if dt == F32 else "st16")
            if func is not None:
                P.add("act", lambda e: e.activation(out=st[0:m, :], in_=bank[0:m, :], func=func), r=[bkey], w=[sk])
            else:
                self.evac(st[0:m, :], bank[0:m, :], [bkey], [sk], scale=scale)
            self.store(dst_rows[ci * 128:ci * 128 + m, t0:t0 + TT], st[0:m, :], [sk])
        return consume

    def tm_store(self, dst, t0, c_dst0):
        def consume(tg, b0, nb, bank, bkey):
            st, sk = self.rot("st16")
            self.evac(st[:, 0:nb], bank[:, 0:nb], [bkey], [sk])
            self.store(dst[t0 + tg * 128:t0 + (tg + 1) * 128, c_dst0 + b0:c_dst0 + b0 + nb], st[:, 0:nb], [sk])
        return consume

    def phase_inproj_gla(self, li, xsrc):
        P = self.P
        self._inproj_setup()
        A = self.A
        glrT = A.tile(TT, BF16)
        etmp = A.tile(TT)
        w2d = self.wb["gla_w_in"][li]
        P.add("pool", lambda e: e.dma_start(out=self.wg[0:16, :], in_=self.wg_d[li]), w=["wg"], dma=True)
        self.prep_x(xsrc, 0, "nmix", li * 16, load=True, norm=True)
        for tt in range(NT):
            t0 = tt * TT
            b = tt % 2
            At, AK = self.Ats[b], ("A", b)
            if tt + 1 < NT:
                self.prep_x(xsrc, tt + 1, "nmix", li * 16, load=True, norm=False)
            self.gemm_fm(At, AK, KC, w2d, 0, 1024, self.fm_store(self.QT, t0, F32, scale=256 ** -0.5))
            self.gemm_fm(At, AK, KC, w2d, 1024, 1024, self.fm_store(self.KT, t0, F32))
            if tt + 1 < NT:
                self.prep_x(xsrc, tt + 1, "nmix", li * 16, load=False, norm=True)
            self.gemm_tm(At, AK, KC, w2d, 2048, 1536, self.tm_store(self.VTM, t0, 0))

            def glr_consume(ci, bank, bkey, m):
                self.evac(glrT[0:16, :], bank[0:16, :], [bkey], ["glrT"])
            self.gemm_fm(At, AK, KC, w2d, 3584, 16, glr_consume)
            for c in range(8):
                bank, bkey = self.ps()
                P.add("pe", lambda e, o=bank, l=self.wg[0:16, c * 128:(c + 1) * 128]: e.matmul(o, l, glrT[0:16, :], start=True, stop=True),
                      r=["wg", "glrT"], w=[bkey])
                nb = self.negbg[:, li * 8 + c:li * 8 + c + 1]
                P.add("act", lambda e, o=bank, nb=nb: e.activation(out=etmp, in_=o, func=AF.Exp, bias=nb, scale=-1.0),
                      r=[bkey, "negbg"], w=["etmp"])
                P.add("act", lambda e: e.activation(out=etmp, in_=etmp, func=AF.Ln, bias=1.0), r=["etmp"], w=["etmp"])
                st, sk = self.rot("st32")
                P.add("dve", lambda e, st=st: e.tensor_scalar(out=st, in0=etmp, scalar1=-1.0 / 16.0, scalar2=None, op0=ALU.mult),
                      r=["etmp"], w=[sk])
                self.store(self.GT[c * 128:(c + 1) * 128, t0:t0 + TT], st, [sk])
            self.gemm_fm(At, AK, KC, w2d, 3600, 1536, self.fm_store(self.OGT, t0, F32, func=AF.Silu))
            self.gemm_fm(At, AK, KC, w2d, 5136, 512, self.fm_store(self.MQT, t0, BF16, scale=128 ** -0.5))
        P.barrier()

    def phase_inproj_fox(self, j, li, xsrc):
        P = self.P
        self._inproj_setup()
        w2d = self.wb["fox_w_in"][j]
        self.prep_x(xsrc, 0, "nmix", li * 16, load=True, norm=True)
        for tt in range(NT):
            t0 = tt * TT
            b = tt % 2
            At, AK = self.Ats[b], ("A", b)
            if tt + 1 < NT:
                self.prep_x(xsrc, tt + 1, "nmix", li * 16, load=True, norm=False)
            self.gemm_fm(At, AK, KC, w2d, 0, 512, self.fm_store(self.FQT, t0, BF16, scale=128 ** -0.5))
            if tt + 1 < NT:
                self.prep_x(xsrc, tt + 1, "nmix", li * 16, load=False, norm=True)
            self.gemm_fm(At, AK, KC, w2d, 512, 1024, self.fm_store(self.FQT[512:1536], t0, BF16, scale=128 ** -0.5))
            self.gemm_fm(At, AK, KC, w2d, 1536, 512, self.fm_store(self.MQT, t0, BF16, scale=128 ** -0.5))
        P.barrier()

    def finish(self):
        from contextlib import ExitStack
        for n, dst in self.debug_outs.items():
            src = self.scratch[n]
            if len(src.shape) == 3:
                for i in range(src.shape[0]):
                    self.P.add("sp", lambda e, d=dst[i], s=src[i]: e.dma_start(out=d, in_=s), dma=True)
            else:
                rows = src.shape[0]
                step = max(1, rows // 8)
                for r0 in range(0, rows, step):
                    self.P.add("sp", lambda e, d=dst[r0:r0 + step], s=src[r0:r0 + step]: e.dma_start(out=d, in_=s), dma=True)
        self.P.barrier()
        with ExitStack() as st:
            self.P.emit(st)
        return self.nc


def make_in_maps(inputs, cores=8):
    cols = pack_cols(inputs)
    consts = make_consts()
    shared = {"cols": cols, "consts": consts,
              "gla_w_gate_up": np.ascontiguousarray(inputs["gla_w_gate_up"], dtype=np.float32)}
    for n, _ in WEIGHTS:
        shared[n] = np.ascontiguousarray(inputs[n], dtype=np.float32)
    maps = []
    for b in range(cores):
        m = dict(shared)
        m["xT"] = np.ascontiguousarray(np.asarray(inputs["x"][b], np.float32).T)
        m["memT"] = np.ascontiguousarray(np.asarray(inputs["mem"][b], np.float32).T)
        maps.append(m)
    return maps


def _attn_methods():
    pass


class Builder2(Builder):
    def attn_tile(self, items, qT, qkey, t0q, out_rows, t0, augq=None):
        P = self.P
        pv, pvk = self.ps("pv")
        dn, dnk = self.ps("dn")
        n = len(items)
        pts = [None] * n

        def qk(i):
            it = items[i]
            o = it["o"]
            sb, sk = self.ps("s")
            it["sb"], it["sk"] = sb, sk
            diag = o is not None
            o = o or 0
            aug = it.get("aug")
            P.add("pe", lambda e, out=sb[:, o:TT], l=it["kT"], rr=qT[:, t0q + o:t0q + TT], sp=(aug is None and not diag):
                  e.matmul(out, l, rr, start=True, stop=sp), r=it["kkeys"] + [qkey], w=[sk])
            if aug is not None:
                P.add("pe", lambda e, out=sb[:, o:TT], l=aug, rr=augq[0:6, t0q + o:t0q + TT], sp=(not diag):
                      e.matmul(out, l, rr, start=False, stop=sp), r=it["kkeys"] + [qkey], w=[sk])
            if diag:
                P.add("pe", lambda e, out=sb[:, o:o + 128]: e.matmul(out, self.identb, self.negb, start=False, stop=True),
                      r=[], w=[sk])
            pt, ptk = self.rot("pt")
            pts[i] = (pt, ptk)
            P.add("act", lambda e, out=pt[:, o:TT], a=sb[:, o:TT]: e.activation(out=out, in_=a, func=AF.Exp), r=[sk], w=[ptk])

        def pvd(i):
            it = items[i]
            o = it["o"] or 0
            pt, ptk = pts[i]
            P.add("pe", lambda e, out=pv[:, o:TT], l=it["v"], rr=pt[:, o:TT], st=(i == 0), sp=(i == n - 1):
                  e.matmul(out, l, rr, start=st, stop=sp), r=it["kkeys"] + [ptk], w=[pvk])
            P.add("pe", lambda e, out=dn[:, o:TT], rr=pt[:, o:TT], st=(i == 0), sp=(i == n - 1):
                  e.matmul(out, self.onesb, rr, start=st, stop=sp), r=[ptk], w=[dnk])
        LAG = 2
        for i in range(n + LAG):
            if i < n:
                qk(i)
            if i >= LAG:
                pvd(i - LAG)
        rc, rck = self.rot("rc")
        P.add("dve", lambda e: e.reciprocal(out=rc, in_=dn), r=[dnk], w=[rck])
        st, sk = self.rot("st16")
        P.add("dve", lambda e: e.tensor_tensor(out=st, in0=pv, in1=rc, op=ALU.mult), r=[pvk, rck], w=[sk])
        self.store(self.MIXT[out_rows:out_rows + 128, t0:t0 + TT], st, [sk])

    def _attn_setup(self):
        A = self.A
        A.reset(self.base)
        self.ps_cfg = {"main": list(range(8)), "pv": [0, 1], "dn": [2, 3], "s": [4, 5, 6, 7]}
        self.mkpool("pt", 4, 512, BF16)
        self.mkpool("rc", 2, 512)
        self.mkpool("st16", 3, 512, BF16)

    def phase_mem_attn(self):
        self._attn_setup()
        A, P = self.A, self.P
        self.mkpool("mq", 2, S, BF16)
        for h in range(4):
            qT, qk = self.rot("mq")
            self.load(qT, self.MQT[h * 128:(h + 1) * 128, :], [qk])
            items = [dict(kT=self.mkT[:, h, jc * 128:(jc + 1) * 128], kkeys=[], v=self.mv[:, jc, h * 128:(h + 1) * 128], o=None)
                     for jc in range(2)]
            for tt in range(NT):
                its = [dict(it) for it in items]
                self.attn_tile(its, qT, qk, tt * TT, 1536 + h * 128, tt * TT)
        P.barrier()

    def phase_fox_attn(self):
        self._attn_setup()
        A, P = self.A, self.P
        self.mkpool("fq", 2, S, BF16)
        self.mkpool("fk", 2, S, BF16)
        self.mkpool("fv", 2, (32, 128), BF16)
        self.mkpool("aq", 2, S, BF16)
        self.mkpool("ak", 2, S, BF16)
        for h in range(12):
            qT, qk = self.rot("fq")
            kT, kk = self.rot("fk")
            vt, vk = self.rot("fv")
            aq, aqk = self.rot("aq")
            ak, akk = self.rot("ak")
            self.load(qT, self.FQT[h * 128:(h + 1) * 128, :], [qk])
            self.load(kT, self.FKT[h * 128:(h + 1) * 128, :], [kk])
            vsrc = self.FVTM[:, h * 128:(h + 1) * 128].rearrange("(c p) d -> p c d", p=128)
            for c0 in range(0, 32, 8):
                self.load(vt[:, c0:c0 + 8, :], vsrc[:, c0:c0 + 8, :], [(vk, c0)])
            self.load(aq[0:6, :], self.AUGQ[h], [aqk])
            self.load(ak[0:6, :], self.AUGK[h], [akk])
            for tt in range(NT):
                items = []
                for i in range(4 * tt + 4):
                    o = None if i < 4 * tt else (i - 4 * tt) * 128
                    items.append(dict(kT=kT[:, i * 128:(i + 1) * 128], kkeys=[kk, (vk, i // 8 * 8), akk, aqk],
                                      v=vt[:, i, :], aug=ak[0:6, i * 128:(i + 1) * 128], o=o))
                self.attn_tile(items, qT, qk, tt * TT, h * 128, tt * TT, augq=aq)
        P.barrier()

    def phase_gla(self, li):
        A, P = self.A, self.P
        A.reset(self.base)
        self.ps_cfg = {"main": [0], "tr": [1, 2], "att": [3], "o": [4, 5], "st": [6, 7]}
        S32 = A.tile((4, 2, 384))
        Sb = A.tile((4, 2, 384), BF16)
        Obuf = A.tile((4, 3, TT))
        og = A.tile((12, TT))
        qtl = [A.tile((2, TT), BF16) for _ in range(4)]
        ktl = [A.tile((2, TT), BF16) for _ in range(4)]
        khl = [A.tile((2, TT), BF16) for _ in range(4)]
        ell = [A.tile((2, 8)) for _ in range(4)]
        vcl = [A.tile((8, 384), BF16) for _ in range(4)]
        self.mkpool("q32", 2, (2, TT))
        self.mkpool("k32", 2, (2, TT))
        self.mkpool("g32", 2, (2, TT))
        self.mkpool("E", 2, (2, TT))
        self.mkpool("Ei", 2, (2, TT))
        self.mkpool("khT", 3, 256, BF16)
        self.mkpool("attb", 3, 64, BF16)
        self.mkpool("sq", 3, 512, BF16)
        self.mkpool("rstd", 2, 512)
        self.mkpool("y", 2, 512)
        self.mkpool("st16", 3, 512, BF16)
        P.add("pool", lambda e: e.memset(S32, 0.0), w=[("S32", h, k) for h in range(4) for k in range(2)])
        P.add("pool", lambda e: e.memset(Sb, 0.0), w=[("Sb", h, k) for h in range(4) for k in range(2)])

        def stage_a(c, h):
            cs = slice(c * 64, (c + 1) * 64)
            trb, trk = self.ps("tr")
            trv = trb.bitcast(BF16)
            for kc in range(2):
                P.add("pe", lambda e, o=trv[0:64, kc * 128:(kc + 1) * 128], a=khl[h][:, kc, cs]: e.transpose(o, a, self.identb),
                      r=[("kh", h, kc)], w=[trk])
            khT, khTk = self.rot("khT")
            P.add("act", lambda e, o=khT[0:64, :], a=trv[0:64, 0:256]: e.activation(out=o, in_=a, func=AF.Copy), r=[trk], w=[khTk])
            ab, abk = self.ps("att")
            for kc in range(2):
                P.add("pe", lambda e, o=ab[0:64, 0:64], l=ktl[h][:, kc, cs], rr=qtl[h][:, kc, cs], st=(kc == 0), sp=(kc == 1):
                      e.matmul(o, l, rr, start=st, stop=sp), r=[("kt", h, kc), ("qt", h, kc)], w=[abk])
            attb, attk = self.rot("attb")
            P.add("dve", lambda e, o=attb[0:64, :], a=ab[0:64, 0:64]: e.tensor_tensor(out=o, in0=a, in1=self.trib[0:64, :], op=ALU.mult),
                  r=[abk], w=[attk])
            return khT, khTk, attb, attk

        def stage_b(c, h, khT, khTk, attb, attk):
            cs = slice(c * 64, (c + 1) * 64)
            ob, obk = self.ps("o")
            for vc in range(3):
                for kc in range(2):
                    P.add("pe", lambda e, o=ob[:, vc * 64:(vc + 1) * 64], l=Sb[:, h, kc, vc * 128:(vc + 1) * 128], rr=qtl[h][:, kc, cs],
                          st=(kc == 0): e.matmul(o, l, rr, start=st, stop=False),
                          r=[("Sb", h, kc), ("qt", h, kc)], w=[obk])
                P.add("pe", lambda e, o=ob[:, vc * 64:(vc + 1) * 64], l=vcl[h][0:64, c, vc * 128:(vc + 1) * 128], rr=attb[0:64, :]:
                      e.matmul(o, l, rr, start=False, stop=True), r=[("vc", h), attk], w=[obk])
            P.add("act", lambda e, o=Obuf[:, h, :, cs], a=ob[:, 0:192].rearrange("p (v t) -> p v t", t=64):
                  e.activation(out=o, in_=a, func=AF.Copy), r=[obk], w=[("O", h, c)])
            for kc in range(2):
                sb_, sbk = self.ps("st")
                P.add("pe", lambda e, o=sb_[:, 0:384], l=khT[0:64, kc * 128:(kc + 1) * 128], rr=vcl[h][0:64, c, :]:
                      e.matmul(o, l, rr, start=True, stop=True), r=[khTk, ("vc", h)], w=[sbk])
                P.add("dve", lambda e, o=S32[:, h, kc, :], sc=ell[h][:, kc, c:c + 1], a=sb_[:, 0:384]:
                      e.scalar_tensor_tensor(out=o, in0=o, scalar=sc, in1=a, op0=ALU.mult, op1=ALU.add),
                      r=[sbk, ("el", h, kc), ("S32", h, kc)], w=[("S32", h, kc)])
                P.add("act", lambda e, o=Sb[:, h, kc, :], a=S32[:, h, kc, :]: e.activation(out=o, in_=a, func=AF.Copy),
                      r=[("S32", h, kc)], w=[("Sb", h, kc)])

        for tt in range(NT):
            t0 = tt * TT
            self.pump(4)
            for h in range(4):
                q32, qk_ = self.rot("q32")
                k32, kk_ = self.rot("k32")
                g32, gk_ = self.rot("g32")
                E, Ek = self.rot("E")
                Ei, Eik = self.rot("Ei")
                rows = slice(h * 256, (h + 1) * 256)
                self.load(q32, self.QT[rows, t0:t0 + TT].rearrange("(kc p) t -> p kc t", p=128), [qk_])
                self.load(k32, self.KT[rows, t0:t0 + TT].rearrange("(kc p) t -> p kc t", p=128), [(kk_, 0), (kk_, 1)])
                self.load(g32, self.GT[rows, t0:t0 + TT].rearrange("(kc p) t -> p kc t", p=128), [gk_])
                self.load(vcl[h][0:64, :, :], self.VTM[t0:t0 + TT, h * 384:(h + 1) * 384].rearrange("(c p) f -> p c f", p=64),
                          [("vc", h)])
                for kc in range(2):
                    P.add("dve", lambda e, b=g32[:, kc, :]: e.tensor_tensor_scan(out=b, data0=self.scanmask, data1=b, initial=0.0,
                                                                              op0=ALU.mult, op1=ALU.add), r=[gk_], w=[gk_])
                    P.add("act", lambda e, o=E[:, kc, :], b=g32[:, kc, :]: e.activation(out=o, in_=b, func=AF.Exp), r=[gk_], w=[(Ek, kc)])
                    P.add("act", lambda e, o=Ei[:, kc, :], b=g32[:, kc, :]: e.activation(out=o, in_=b, func=AF.Exp, scale=-1.0),
                          r=[gk_], w=[(Eik, kc)])
                    P.add("dve", lambda e, o=qtl[h][:, kc, :], a=q32[:, kc, :], b=E[:, kc, :]: e.tensor_tensor(out=o, in0=a, in1=b, op=ALU.mult),
                          r=[qk_, (Ek, kc)], w=[("qt", h, kc)])
                    P.add("dve", lambda e, o=k32[:, kc, :], b=Ei[:, kc, :]: e.tensor_tensor(out=o, in0=o, in1=b, op=ALU.mult),
                          r=[(Eik, kc), (kk_, kc)], w=[(kk_, kc)])
                    P.add("act", lambda e, o=ktl[h][:, kc, :], a=k32[:, kc, :]: e.activation(out=o, in_=a, func=AF.Copy),
                          r=[(kk_, kc)], w=[("kt", h, kc)])
                    elv = E[:, kc, :].rearrange("p (c t) -> p c t", t=64)[:, :, 63]
                    P.add("pool", lambda e, o=ell[h][:, kc, :], a=elv: e.tensor_copy(out=o, in_=a), r=[(Ek, kc)], w=[("el", h, kc)])
                    elb = E[:, kc, :].rearrange("p (c t) -> p c t", t=64)[:, :, 63:64].to_broadcast([128, 8, 64])
                    P.add("pool", lambda e, o=khl[h][:, kc, :].rearrange("p (c t) -> p c t", t=64),
                          a=k32[:, kc, :].rearrange("p (c t) -> p c t", t=64), b=elb: e.tensor_tensor(out=o, in0=a, in1=b, op=ALU.mult),
                          r=[(kk_, kc), (Ek, kc)], w=[("kh", h, kc)])
            items = [(c, h) for c in range(8) for h in range(4)]
            pend = None
            for it in items:
                a_out = stage_a(*it)
                if pend is not None:
                    stage_b(*pend)
                pend = it + a_out
            stage_b(*pend)
            ogv = self.OGT[:, t0:t0 + TT].rearrange("(c p) t -> p c t", p=128)
            for c0 in range(0, 12, 4):
                self.load(og[:, c0:c0 + 4, :], ogv[:, c0:c0 + 4, :], [("og", c0)])
            for h in range(4):
                rstd, rk = self.rot("rstd")
                okeys = [("O", h, c) for c in range(8)]
                self.rstd_from([(Obuf[:, h, vc, :], okeys) for vc in range(3)], 384, rstd, rk)
                for vc in range(3):
                    y, yk = self.rot("y")
                    ci = h * 3 + vc
                    P.add("dve", lambda e, o=y, a=Obuf[:, h, vc, :], g=self.col("gnorm", li * 12 + ci), rs=rstd:
                          e.scalar_tensor_tensor(out=o, in0=a, scalar=g, in1=rs, op0=ALU.mult, op1=ALU.mult),
                          r=okeys + [rk], w=[yk])
                    st, sk = self.rot("st16")
                    P.add("pool", lambda e, o=st, a=y, b=og[:, ci, :]: e.tensor_tensor(out=o, in0=a, in1=b, op=ALU.mult),
                          r=[yk, ("og", ci // 4 * 4)], w=[sk])
                    self.store(self.MIXT[ci * 128:(ci + 1) * 128, t0:t0 + TT], st, [sk])
        P.barrier()

    def phase_out_ffn(self, li, wout2d, xsrc, final=False):
        A, P = self.A, self.P
        A.reset(self.base)
        self.ps_cfg = {"main": list(range(8))}
        xt = A.tile((KC, TT))
        At = A.tile((KC, TT), BF16)
        hid = A.tile((44, TT), BF16)
        rstd = A.tile(TT)
        halo = A.tile((88, 2))
        self.mk_wpool(3)
        self.mkpool("sq", 3, 512, BF16)
        self.mkpool("Ua", 2, 514)
        self.mkpool("Uv", 2, 514)
        self.mkpool("acca", 2, 512)
        self.mkpool("accv", 2, 512)
        self.mkpool("sil", 2, 512)
        self.mkpool("st32", 3, 512)
        wup = self.wb["ffn_w_up"][li]
        wdn = self.wb["ffn_w_down"][li]
        P.add("pool", lambda e: e.memset(halo, 0.0), w=[("halo", c) for c in range(88)])
        cw = COLS["convw"] + li * 3 * 88
        cb = COLS["convb"] + li * 88

        def resid(ci, bank, bkey, m):
            P.add("dve", lambda e, o=xt[:, ci, :], b=bank: e.tensor_tensor(out=o, in0=o, in1=b, op=ALU.add),
                  r=[bkey, ("x", ci)], w=[("x", ci)])

        def conv(bank, bkey, cc, nm):
            U, Uk = self.rot("U" + nm)
            acc, ak = self.rot("acc" + nm)
            P.add("pool", lambda e, o=U[:, 0:2], a=halo[:, cc, :]: e.tensor_copy(out=o, in_=a), r=[("halo", cc)], w=[(Uk, "h")])
            P.add("act", lambda e, o=U[:, 2:514], a=bank: e.activation(out=o, in_=a, func=AF.Copy), r=[bkey], w=[(Uk, "b")])
            P.add("pool", lambda e, o=halo[:, cc, :], a=U[:, 512:514]: e.tensor_copy(out=o, in_=a), r=[(Uk, "b")], w=[("halo", cc)])
            w0 = self.cols[:, cw + 0 * 88 + cc:cw + 0 * 88 + cc + 1]
            w1 = self.cols[:, cw + 1 * 88 + cc:cw + 1 * 88 + cc + 1]
            w2 = self.cols[:, cw + 2 * 88 + cc:cw + 2 * 88 + cc + 1]
            bb = self.cols[:, cb + cc:cb + cc + 1]
            P.add("dve", lambda e, o=acc, a=U[:, 2:514]: e.tensor_scalar(out=o, in0=a, scalar1=w2, scalar2=bb, op0=ALU.mult, op1=ALU.add),
                  r=[(Uk, "b")], w=[ak])
            P.add("dve", lambda e, o=acc, a=U[:, 1:513]: e.scalar_tensor_tensor(out=o, in0=a, scalar=w1, in1=o, op0=ALU.mult, op1=ALU.add),
                  r=[(Uk, "b"), (Uk, "h"), ak], w=[ak])
            P.add("dve", lambda e, o=acc, a=U[:, 0:512]: e.scalar_tensor_tensor(out=o, in0=a, scalar=w0, in1=o, op0=ALU.mult, op1=ALU.add),
                  r=[(Uk, "b"), (Uk, "h"), ak], w=[ak])
            return acc, ak

        for tt in range(NT):
            t0 = tt * TT
            self.load_x(xt, xsrc, t0)
            mv_ = self.MIXT[:, t0:t0 + TT].rearrange("(kc p) t -> p kc t", p=128)
            for k0 in range(0, KC, 4):
                self.load(At[:, k0:k0 + 4, :], mv_[:, k0:k0 + 4, :], [("A", k) for k in range(k0, k0 + 4)])
            self.gemm_fm(At, "A", KC, wout2d, 0, D, resid)
            self.norm_to_A(xt, At, "nffn", li * 16, rstd)
            for hb in range(11):
                wa, wak = self.wload(wup, 0, KC, hb * 512, 512)
                wv, wvk = self.wload(wup, 0, KC, FFN_H + hb * 512, 512)
                for ch in range(4):
                    c = hb * 4 + ch
                    ba, bak = self.ps()
                    for k in range(KC):
                        P.add("pe", lambda e, o=ba, l=wa[:, k, ch * 128:(ch + 1) * 128], rr=At[:, k, :], st=(k == 0), sp=(k == KC - 1):
                              e.matmul(o, l, rr, start=st, stop=sp), r=[wak, ("A", k)], w=[bak])
                    bv, bvk = self.ps()
                    for k in range(KC):
                        P.add("pe", lambda e, o=bv, l=wv[:, k, ch * 128:(ch + 1) * 128], rr=At[:, k, :], st=(k == 0), sp=(k == KC - 1):
                              e.matmul(o, l, rr, start=st, stop=sp), r=[wvk, ("A", k)], w=[bvk])
                    acca, aak = conv(ba, bak, c, "a")
                    accv, avk = conv(bv, bvk, 44 + c, "v")
                    sil, sk = self.rot("sil")
                    P.add("act", lambda e, o=sil, a=acca: e.activation(out=o, in_=a, func=AF.Silu), r=[aak], w=[sk])
                    P.add("pool", lambda e, o=hid[:, c, :], a=sil, b=accv: e.tensor_tensor(out=o, in0=a, in1=b, op=ALU.mult),
                          r=[sk, avk], w=[("hid", c)])
            self.gemm_fm(hid, "hid", 44, wdn, 0, D, resid, kblock=11)
            if not final:
                self.store_x(xt, self.XT, t0)
            else:
                self.rstd_from([(xt[:, k, :], ("x", k)) for k in range(KC)], D, rstd, "rstd")
                ov = self.outT[:, t0:t0 + TT].rearrange("(kc p) t -> p kc t", p=128)
                for k in range(KC):
                    st, sk = self.rot("st32")
                    P.add("dve", lambda e, o=st, a=xt[:, k, :], g=self.col("nfin", k):
                          e.scalar_tensor_tensor(out=o, in0=a, scalar=g, in1=rstd, op0=ALU.mult, op1=ALU.mult),
                          r=[("x", k), "rstd"], w=[sk])
                    self.store(ov[:, k, :], st, [sk])
        P.barrier()

    def phase_fox_kv(self, xsrc):
        P = self.P
        self._inproj_setup()
        A = self.A
        lf = A.tile(S)
        cb1 = A.tile(S, BF16)
        cb2 = A.tile(S, BF16)
        etmp = A.tile(TT)
        onesS = cb2
        w2d = self.wb["fox_w_kv"]
        self.prep_x(xsrc, 0, "nfkv", 0, load=True, norm=True)
        for tt in range(NT):
            t0 = tt * TT
            b = tt % 2
            At, AK = self.Ats[b], ("A", b)
            if tt + 1 < NT:
                self.prep_x(xsrc, tt + 1, "nfkv", 0, load=True, norm=False)
            self.gemm_fm(At, AK, KC, w2d, 0, 1536, self.fm_store(self.FKT, t0, BF16))
            if tt + 1 < NT:
                self.prep_x(xsrc, tt + 1, "nfkv", 0, load=False, norm=True)
            self.gemm_tm(At, AK, KC, w2d, 1536, 1536, self.tm_store(self.FVTM, t0, 0))

            def fl_consume(ci, bank, bkey, m, t0=t0):
                P.add("act", lambda e: e.activation(out=etmp[0:12, :], in_=bank[0:12, :], func=AF.Exp, bias=self.negbf[0:12, :], scale=-1.0),
                      r=[bkey], w=["etmp"])
                P.add("act", lambda e: e.activation(out=etmp[0:12, :], in_=etmp[0:12, :], func=AF.Ln, bias=1.0), r=["etmp"], w=["etmp"])
                P.add("dve", lambda e: e.tensor_scalar(out=lf[0:12, t0:t0 + TT], in0=etmp[0:12, :], scalar1=-1.0, scalar2=None, op0=ALU.mult),
                      r=["etmp"], w=["lf"])
            self.gemm_fm(At, AK, KC, w2d, 3072, 12, fl_consume)
        for tt in range(NT):
            t0 = tt * TT
            init = 0.0 if tt == 0 else lf[0:12, t0 - 1:t0]
            P.add("dve", lambda e, o=lf[0:12, t0:t0 + TT], init=init: e.tensor_tensor_scan(out=o, data0=self.ones512[0:12, :], data1=o, initial=init,
                                                                                       op0=ALU.mult, op1=ALU.add), r=["lf"], w=["lf"])
        P.add("pool", lambda e: e.memset(onesS[0:12, :], 1.0), w=["cb2"])
        for r_ in range(3):
            self.store(self.AUGQ[:, 3 + r_, :], onesS[0:12, :], ["cb2"])
            self.store(self.AUGK[:, r_, :], onesS[0:12, :], ["cb2"])
        for r_ in range(3):
            P.add("dve", lambda e: e.tensor_copy(out=cb1[0:12, :], in_=lf[0:12, :]), r=["lf"], w=["cb1"])
            self.store(self.AUGQ[:, r_, :], cb1[0:12, :], ["cb1"])
            if r_ < 2:
                P.add("dve", lambda e: e.tensor_tensor(out=lf[0:12, :], in0=lf[0:12, :], in1=cb1[0:12, :], op=ALU.subtract), r=["lf", "cb1"], w=["lf"])
            P.add("dve", lambda e: e.tensor_scalar(out=cb2[0:12, :], in0=cb1[0:12, :], scalar1=-1.0, scalar2=None, op0=ALU.mult),
                  r=["cb1"], w=["cb2"])
            self.store(self.AUGK[:, 3 + r_, :], cb2[0:12, :], ["cb2"])
        P.barrier()


def build_full(upto=99, debug_outs=()):
    B = Builder2(debug_outs=debug_outs)
    B.ensure("gla_w_in", 0)
    B.phase_mem_prep()
    step = 0
    xsrc = B.xT_in
    for li in range(4):
        B.phase_mem_kv(li)
        if li < 2:
            B.ensure("gla_w_in", li)
            B.phase_inproj_gla(li, xsrc)
            step += 1
            if step >= upto:
                break
            B.phase_gla(li)
            wout = B.wb["gla_w_out"][li]
        else:
            if li == 2:
                B.ensure("fox_w_kv", None)
                B.phase_fox_kv(xsrc)
            B.ensure("fox_w_in", li - 2)
            B.phase_inproj_fox(li - 2, li, xsrc)
            step += 1
            if step >= upto:
                break
            B.phase_fox_attn()
            wout = B.wb["fox_w_out"][li - 2]
        B.phase_mem_attn()
        step += 1
        if step >= upto:
            break
        B.ensure("ffn_w_down", li)
        B.phase_out_ffn(li, wout, xsrc, final=(li == 3))
        xsrc = B.XT
        step += 1
        if step >= upto:
            break
    return B, B.finish()


_CACHE = {}


def kernel(**inputs):
    if "nc" not in _CACHE:
        _CACHE["nc"] = build_full()[1]
    nc = _CACHE["nc"]
    maps = make_in_maps(inputs, cores=8)
    res = run_bass_kernel_spmd(nc, maps, core_ids=list(range(8)))
    out = np.stack([np.ascontiguousarray(np.asarray(r["outT"], np.float32).T) for r in res.results], axis=0)
    return out
```

```python
import numpy as np
import concourse.bass as bass
import concourse.mybir as mybir
from concourse.bass_utils import run_bass_kernel_spmd

F32 = mybir.dt.float32
BF16 = mybir.dt.bfloat16
AF = mybir.ActivationFunctionType
ALU = mybir.AluOpType
AX = mybir.AxisListType

S = 4096
D = 2048
KC = D // 128
TT = 512
NT = S // TT
NMEM = 256
FFN_H = 5632
EPS = 1e-6


def _dsize(dt):
    return 4 if dt == F32 else 2


class Op:
    __slots__ = ("eng", "fn", "deps", "dma", "sig", "tok", "waits", "slot", "q", "nobar")


class Prog:
    COMPUTE = ("pe", "act", "dve", "pool")
    ALL = ("pe", "act", "dve", "pool", "sp")
    QUEUES = {"sp": ("sp", 14), "pool": ("pool", 12), "poolc": ("pool", 4), "act": ("act", 4)}

    def __init__(self, nc):
        self.nc = nc
        self.ops = []
        self.lw = {}
        self.rd = {}
        self.last = {e: None for e in self.ALL}
        self.dmas = {q: [] for q in self.QUEUES}
        self.persist = {}

    def add(self, eng, fn, r=(), w=(), dma=False, nobar=False):
        op = Op()
        op.q = eng
        op.nobar = nobar
        if dma:
            eng = self.QUEUES[op.q][0]
        op.eng, op.fn, op.dma, op.sig, op.tok, op.slot = eng, fn, dma, False, None, None
        deps = []
        for k in r:
            o = self.lw.get(k)
            if o is not None:
                deps.append(o)
        for k in w:
            o = self.lw.get(k)
            if o is not None:
                deps.append(o)
            rk = self.rd.get(k)
            if rk:
                deps.extend(rk[0].values())
                deps.extend(rk[1])
        if dma:
            q = self.dmas[op.q]
            ns = self.QUEUES[op.q][1]
            if len(q) >= ns:
                deps.append(q[-ns])
            q.append(op)
        op.deps = deps
        for k in r:
            rk = self.rd.get(k)
            if rk is None:
                rk = self.rd[k] = ({}, [])
            if dma:
                rk[1].append(op)
            else:
                rk[0][eng] = op
        for k in w:
            self.lw[k] = op
            self.rd[k] = ({}, [])
            if nobar:
                self.persist[k] = op
        if not dma:
            self.last[eng] = op
        self.ops.append(op)
        return op

    def barrier(self):
        deps = [o for o in self.last.values() if o is not None]
        for qn, (e, ns) in self.QUEUES.items():
            deps.extend([o for o in self.dmas[qn] if not o.nobar][-ns:])
        for e in self.ALL:
            op = Op()
            op.eng, op.fn, op.dma, op.sig, op.tok, op.slot, op.q, op.nobar = e, None, False, False, None, None, e, False
            op.deps = list(deps)
            self.ops.append(op)
        self.lw = dict(self.persist)
        self.rd.clear()

    def emit(self, stack):
        nc = self.nc
        sems = {e: stack.enter_context(nc.semaphore("s_" + e)) for e in self.COMPUTE}
        dsem = {qn: [stack.enter_context(nc.semaphore("d_%s%d" % (qn, i))) for i in range(ns)]
                for qn, (e, ns) in self.QUEUES.items()}
        for op in self.ops:
            for d in op.deps:
                if d.dma:
                    continue
                if d.eng == op.eng and op.eng == "pe" and not op.dma:
                    continue
                d.sig = True
        cnt = {e: 0 for e in self.COMPUTE}
        dcnt = {qn: 0 for qn in self.QUEUES}
        for op in self.ops:
            if op.dma:
                i = dcnt[op.q]
                dcnt[op.q] += 1
                ns = self.QUEUES[op.q][1]
                op.slot = dsem[op.q][i % ns]
                op.tok = 16 * (i // ns + 1)
            elif op.sig:
                cnt[op.eng] += 1
                op.tok = cnt[op.eng]
        seen = {e: {} for e in self.ALL}
        per = {e: [] for e in self.ALL}
        for op in self.ops:
            sn = seen[op.eng]
            waits = {}
            for d in op.deps:
                if d.dma:
                    sem, val = d.slot, d.tok
                else:
                    if d.eng == op.eng and op.eng == "pe" and not op.dma:
                        continue
                    sem, val = sems[d.eng], d.tok
                key = id(sem)
                if sn.get(key, 0) >= val:
                    continue
                if key not in waits or waits[key][1] < val:
                    waits[key] = (sem, val)
            for key, (sem, val) in waits.items():
                sn[key] = val
            op.waits = list(waits.values())
            per[op.eng].append(op)
        self.n_ops = {e: len(per[e]) for e in self.ALL}

        def run(ename, e):
            for op in per[ename]:
                for sem, val in op.waits:
                    e.wait_ge(sem, val)
                if op.fn is None:
                    continue
                ins = op.fn(e)
                if op.dma:
                    ins.then_inc(op.slot, 16)
                elif op.sig:
                    ins.then_inc(sems[ename], 1)

        block = stack.enter_context(nc.Block())

        @block.tensor
        def _(e):
            run("pe", e)

        @block.scalar
        def _(e):
            run("act", e)

        @block.vector
        def _(e):
            run("dve", e)

        @block.gpsimd
        def _(e):
            run("pool", e)

        @block.sync
        def _(e):
            run("sp", e)


class Arena:
    def __init__(self, ap, nwords):
        self.ap = ap
        self.n = nwords
        self.off = 0

    def reset(self, off=0):
        self.off = off

    def tile(self, free, dt=F32, parts=128):
        if isinstance(free, int):
            free = (free,)
        n = 1
        for f in free:
            n *= f
        words = (n * _dsize(dt) + 3) // 4
        words = (words + 7) // 8 * 8
        assert self.off + words <= self.n, ("SBUF arena overflow", self.off, words, self.n)
        a = self.ap[0:parts, self.off:self.off + words]
        self.off += words
        if dt != F32:
            a = a.bitcast(dt)
        a = a[:, 0:n]
        if len(free) == 2:
            a = a.rearrange("p (a b) -> p a b", b=free[1])
        elif len(free) == 3:
            a = a.rearrange("p (a b c) -> p a b c", b=free[1], c=free[2])
        return a


COLS = {}
_off = 0
for _name, _n in (("nmix", 64), ("nffn", 64), ("nmem", 16), ("nfin", 16), ("nfkv", 16), ("bgate", 16),
                  ("gnorm", 24), ("bf", 1), ("convw", 4 * 3 * 88), ("convb", 4 * 88)):
    COLS[_name] = _off
    _off += _n
NCOLS = (_off + 7) // 8 * 8
GC = 128
C_ID, C_TRI, C_NEG, C_SCAN = 0, 128, 256, 384
NCONST = 384 + 512


def _colify(v):
    v = np.asarray(v, np.float32)
    lead = int(np.prod(v.shape[:-1])) if v.ndim > 1 else 1
    n = v.shape[-1] // 128
    return np.ascontiguousarray(v.reshape(lead, n, 128).transpose(2, 0, 1).reshape(128, lead * n))


def pack_cols(inp):
    c = np.zeros((128, NCOLS), np.float32)

    def put(name, arr):
        c[:, COLS[name]:COLS[name] + arr.shape[1]] = arr
    put("nmix", _colify(inp["norm_mix"]))
    put("nffn", _colify(inp["norm_ffn"]))
    put("nmem", _colify(inp["norm_mem"]))
    put("nfin", _colify(inp["norm_final"]))
    put("nfkv", _colify(inp["fox_kv_norm"]))
    put("bgate", _colify(inp["gla_b_gate"]))
    put("gnorm", _colify(inp["gla_norm"]))
    c[0:12, COLS["bf"]] = np.asarray(inp["fox_b_f"], np.float32)
    put("convw", _colify(inp["ffn_conv_w"]))
    put("convb", _colify(inp["ffn_conv_b"]))
    return c


def make_consts():
    k = np.zeros((128, NCONST), np.float32)
    k[:, C_ID:C_ID + 128] = np.eye(128, dtype=np.float32)
    s = np.arange(128)[:, None]
    t = np.arange(128)[None, :]
    k[:, C_TRI:C_TRI + 128] = (s <= t).astype(np.float32)
    k[:, C_NEG:C_NEG + 128] = np.where(s <= t, 0.0, -30000.0).astype(np.float32)
    m = np.ones(512, np.float32)
    m[::GC] = 0.0
    k[:, C_SCAN:C_SCAN + 512] = m[None, :]
    return k


WEIGHTS = (("mem_w_kv", (4, 2048, 1024)), ("gla_w_in", (2, 2048, 5648)), ("gla_w_out", (2, 2048, 2048)),
           ("fox_w_kv", (2048, 3084)), ("fox_w_in", (2, 2048, 2048)), ("fox_w_out", (2, 2048, 2048)),
           ("ffn_w_up", (4, 2048, 11264)), ("ffn_w_down", (4, 5632, 2048)))


def id_of(ap):
    return (ap.name, int(ap.offset))


class Builder:
    NWP = 4

    def __init__(self, debug_outs=()):
        nc = bass.Bass("TRN2", target_bir_lowering=False)
        self.nc = nc

        def di(name, shape, dt=F32):
            return nc.dram_tensor(name, list(shape), dt, kind="ExternalInput").ap()

        def scr(name, shape, dt):
            return nc.dram_tensor(name, list(shape), dt, kind="Internal").ap()
        self.xT_in = di("xT", [D, S])
        self.memT = di("memT", [D, NMEM])
        self.cols_d = di("cols", [128, NCOLS])
        self.consts_d = di("consts", [128, NCONST])
        self.wg_d = di("gla_w_gate_up", [2, 16, 1024])
        self.w32 = {n: di(n, s) for n, s in WEIGHTS}
        self.outT = nc.dram_tensor("outT", [D, S], F32, kind="ExternalOutput").ap()
        self.wb = {n: scr(n + "_bf", s, BF16) for n, s in WEIGHTS}
        self.XT = scr("XT", [D, S], F32)
        self.QT = scr("QT", [1024, S], F32)
        self.KT = scr("KT", [1024, S], F32)
        self.GT = scr("GT", [1024, S], F32)
        self.OGT = scr("OGT", [1536, S], F32)
        self.VTM = scr("VTM", [S, 1536], BF16)
        self.MQT = scr("MQT", [512, S], BF16)
        self.MIXT = scr("MIXT", [D, S], BF16)
        self.FQT = scr("FQT", [1536, S], BF16)
        self.FKT = scr("FKT", [1536, S], BF16)
        self.FVTM = scr("FVTM", [S, 1536], BF16)
        self.AUGQ = scr("AUGQ", [12, 6, S], BF16)
        self.AUGK = scr("AUGK", [12, 6, S], BF16)
        self.scratch = dict(XT=self.XT, QT=self.QT, KT=self.KT, GT=self.GT, OGT=self.OGT, VTM=self.VTM,
                            MQT=self.MQT, MIXT=self.MIXT, FQT=self.FQT, FKT=self.FKT, FVTM=self.FVTM,
                            AUGQ=self.AUGQ, AUGK=self.AUGK)
        self.debug_outs = {}
        for n in debug_outs:
            a = self.scratch[n]
            self.debug_outs[n] = nc.dram_tensor("dbg_" + n, list(a.shape), a.dtype, kind="ExternalOutput").ap()
        NW = 52000
        self.arena_t = nc.alloc_sbuf_tensor("arena", [128, NW], F32)
        self.A = Arena(self.arena_t.ap(), NW)
        self.banks = [nc.alloc_psum_tensor("ps%d" % i, [128, 512], F32).ap() for i in range(8)]
        self.ps_cfg = {"main": list(range(8))}
        self.ps_cnt = {}
        self.P = Prog(nc)
        self.pools = {}
        self.evi = 0
        self.pump_n = 3
        self.conv_init()
        self._persistent()

    def ps(self, role="main"):
        lst = self.ps_cfg[role]
        i = lst[self.ps_cnt.get(role, 0) % len(lst)]
        self.ps_cnt[role] = self.ps_cnt.get(role, 0) + 1
        return self.banks[i], ("ps", i)

    def mkpool(self, name, n, free, dt=F32):
        self.pools[name] = [[self.A.tile(free, dt) for _ in range(n)], 0]

    def rot(self, name):
        p = self.pools[name]
        i = p[1] % len(p[0])
        p[1] += 1
        return p[0][i], (name, i)

    def col(self, name, idx):
        o = COLS[name] + idx
        return self.cols[:, o:o + 1]

    def evac(self, out, in_, r, w, scale=None, eng=None):
        P = self.P
        if eng is None:
            eng = ("act", "dve")[self.evi % 2]
            self.evi += 1
        if eng == "act":
            if scale is None:
                P.add("act", lambda e: e.activation(out=out, in_=in_, func=AF.Copy), r=r, w=w)
            else:
                P.add("act", lambda e: e.activation(out=out, in_=in_, func=AF.Copy, scale=float(scale)), r=r, w=w)
        else:
            if scale is None:
                P.add(eng, lambda e: e.tensor_copy(out=out, in_=in_), r=r, w=w)
            else:
                P.add(eng, lambda e: e.tensor_scalar(out=out, in0=in_, scalar1=float(scale), scalar2=None,
                                                     op0=ALU.mult), r=r, w=w)

    def store(self, dst, src, r, q="pool"):
        self.P.add(q, lambda e: e.dma_start(out=dst, in_=src), r=r, dma=True)

    def load(self, dst, src, w, q="sp"):
        self.P.add(q, lambda e: e.dma_start(out=dst, in_=src), w=w, dma=True)

    def _persistent(self):
        A, P = self.A, self.P
        self.cols = A.tile(NCOLS)
        self.identb = A.tile(128, BF16)
        self.ones32 = A.tile(128)
        self.onesb = A.tile(128, BF16)
        self.trib = A.tile(128, BF16)
        self.negb = A.tile(128, BF16)
        self.scanmask = A.tile(512)
        self.ones512 = A.tile(512)
        self.epst = A.tile(1)
        self.negbg = A.tile(16)
        self.negbf = A.tile(1)
        self.memnT = A.tile((KC, NMEM), BF16)
        self.mkT = A.tile((4, NMEM), BF16)
        self.mv = A.tile((2, 512), BF16)
        self.wg = A.tile(1024, BF16)
        self.base = A.off
        cst = A.tile(NCONST)
        self.load(self.cols, self.cols_d, ["cols"])
        self.load(cst, self.consts_d, ["cst"])
        P.add("pool", lambda e: e.memset(self.ones32, 1.0), w=["ones32"])
        P.add("pool", lambda e: e.memset(self.onesb, 1.0), w=["onesb"])
        P.add("pool", lambda e: e.memset(self.ones512, 1.0), w=["ones512"])
        P.add("pool", lambda e: e.memset(self.epst, EPS), w=["epst"])
        P.add("dve", lambda e: e.tensor_copy(out=self.identb, in_=cst[:, C_ID:C_ID + 128]), r=["cst"], w=["identb"])
        P.add("dve", lambda e: e.tensor_copy(out=self.trib, in_=cst[:, C_TRI:C_TRI + 128]), r=["cst"], w=["trib"])
        P.add("dve", lambda e: e.tensor_copy(out=self.negb, in_=cst[:, C_NEG:C_NEG + 128]), r=["cst"], w=["negb"])
        P.add("dve", lambda e: e.tensor_copy(out=self.scanmask, in_=cst[:, C_SCAN:C_SCAN + 512]), r=["cst"], w=["scanmask"])
        ob = COLS["bgate"]
        P.add("dve", lambda e: e.tensor_scalar(out=self.negbg, in0=self.cols[:, ob:ob + 16], scalar1=-1.0, scalar2=None,
                                               op0=ALU.mult), r=["cols"], w=["negbg"])
        of = COLS["bf"]
        P.add("dve", lambda e: e.tensor_scalar(out=self.negbf, in0=self.cols[:, of:of + 1], scalar1=-1.0, scalar2=None,
                                               op0=ALU.mult), r=["cols"], w=["negbf"])
        P.barrier()

    def conv_init(self):
        order = [("mem_w_kv", i) for i in range(4)] + [("gla_w_in", 0), ("gla_w_out", 0), ("ffn_w_up", 0), ("ffn_w_down", 0),
                 ("gla_w_in", 1), ("gla_w_out", 1), ("ffn_w_up", 1), ("ffn_w_down", 1), ("fox_w_kv", None), ("fox_w_in", 0),
                 ("fox_w_out", 0), ("ffn_w_up", 2), ("ffn_w_down", 2), ("fox_w_in", 1), ("fox_w_out", 1), ("ffn_w_up", 3),
                 ("ffn_w_down", 3)]
        self.conv_q = []
        self.wkeys = {}
        self.conv_done = {}
        for n, li in order:
            src = self.w32[n] if li is None else self.w32[n][li]
            dst = self.wb[n] if li is None else self.wb[n][li]
            rows, colsn = src.shape
            step = max(1, (1 << 20) // colsn)
            keys = []
            for r0 in range(0, rows, step):
                r1 = min(rows, r0 + step)
                key = ("W", n, li, r0)
                keys.append(key)
                self.conv_q.append((key, src[r0:r1, :], dst[r0:r1, :]))
            self.wkeys[id_of(dst)] = keys
            self.conv_done[(n, li)] = len(self.conv_q)
        self.conv_i = 0

    def pump(self, n):
        while n > 0 and self.conv_i < len(self.conv_q):
            key, s_, d_ = self.conv_q[self.conv_i]
            self.conv_i += 1
            n -= 1
            self.P.add("poolc", lambda e, s_=s_, d_=d_: e.dma_start(out=d_, in_=s_), w=[key], dma=True, nobar=True)

    def ensure(self, n, li):
        self.pump(self.conv_done[(n, li)] - self.conv_i)

    def wload(self, w2d, k0, nk, c0, ncols):
        slot = self.wp_i % self.NWP
        self.wp_i += 1
        t = self.wpool[slot]
        key = ("wp", slot)
        src = w2d[k0 * 128:(k0 + nk) * 128, c0:c0 + ncols].rearrange("(kc p) f -> p kc f", p=128)
        self.P.add("sp", lambda e, d=t[:, 0:nk, 0:ncols]: e.dma_start(out=d, in_=src), r=self.wkeys[id_of(w2d)], w=[key], dma=True)
        return t, key

    def mk_wpool(self, n=4, width=512):
        self.NWP = n
        self.wpool = [self.A.tile((16, width), BF16) for _ in range(n)]
        self.wp_i = 0

    def gemm_fm(self, At, akey, nkc, w2d, c0, ncols, consume, kblock=16, T=TT, bw=512):
        P = self.P
        ci = 0
        for b0 in range(0, ncols, bw):
            nb = min(bw, ncols - b0)
            nch = (nb + 127) // 128
            bks = [self.ps() for _ in range(nch)]
            for k0 in range(0, nkc, kblock):
                nk = min(kblock, nkc - k0)
                wt, wkey = self.wload(w2d, k0, nk, c0 + b0, nb)
                for ch in range(nch):
                    m = min(128, nb - ch * 128)
                    bank, bkey = bks[ch]
                    for k in range(nk):
                        kk = k0 + k
                        P.add("pe", lambda e, o=bank[0:m, 0:T], l=wt[:, k, ch * 128:ch * 128 + m], rr=At[:, kk, 0:T],
                              st=(kk == 0), sp=(kk == nkc - 1): e.matmul(o, l, rr, start=st, stop=sp),
                              r=[wkey, (akey, kk)], w=[bkey])
            for ch in range(nch):
                m = min(128, nb - ch * 128)
                consume(ci, bks[ch][0], bks[ch][1], m)
                ci += 1

    def gemm_tm(self, At, akey, nkc, w2d, c0, ncols, consume, T=TT):
        P = self.P
        for b0 in range(0, ncols, 512):
            nb = min(512, ncols - b0)
            wt, wkey = self.wload(w2d, 0, nkc, c0 + b0, nb)
            for tg in range(T // 128):
                bank, bkey = self.ps()
                for k in range(nkc):
                    P.add("pe", lambda e, o=bank[:, 0:nb], l=At[:, k, tg * 128:(tg + 1) * 128], rr=wt[:, k, 0:nb],
                          st=(k == 0), sp=(k == nkc - 1): e.matmul(o, l, rr, start=st, stop=sp),
                          r=[wkey, (akey, k)], w=[bkey])
                consume(tg, b0, nb, bank, bkey)

    def rstd_from(self, chunks, nfeat, out, okey, T=TT):
        P = self.P
        bank, bkey = self.ps()
        n = len(chunks)
        for i, (xa, xk) in enumerate(chunks):
            sq, sk = self.rot("sq")
            P.add("act", lambda e, o=sq[:, 0:T], a=xa: e.activation(out=o, in_=a, func=AF.Square),
                  r=(xk if isinstance(xk, list) else [xk]), w=[sk])
            P.add("pe", lambda e, o=bank[:, 0:T], rr=sq[:, 0:T], st=(i == 0), sp=(i == n - 1):
                  e.matmul(o, self.onesb, rr, start=st, stop=sp), r=[sk], w=[bkey])
        P.add("act", lambda e: e.activation(out=out, in_=bank[:, 0:T], func=AF.Sqrt, bias=self.epst, scale=1.0 / nfeat),
              r=[bkey], w=[okey])
        P.add("dve", lambda e: e.reciprocal(out=out, in_=out), r=[okey], w=[okey])

    def load_x(self, xt, src, t0, xk="x"):
        self.pump(self.pump_n)
        v = src[:, t0:t0 + TT].rearrange("(kc p) t -> p kc t", p=128)
        for k0 in range(0, KC, 4):
            self.load(xt[:, k0:k0 + 4, :], v[:, k0:k0 + 4, :], [(xk, k) for k in range(k0, k0 + 4)])

    def store_x(self, xt, dst, t0):
        v = dst[:, t0:t0 + TT].rearrange("(kc p) t -> p kc t", p=128)
        for k0 in range(0, KC, 4):
            self.store(v[:, k0:k0 + 4, :], xt[:, k0:k0 + 4, :], [("x", k) for k in range(k0, k0 + 4)])

    def norm_to_A(self, xt, At, gname, gidx0, rstd, xk="x", ak="A", rk="rstd"):
        P = self.P
        self.rstd_from([(xt[:, k, :], (xk, k)) for k in range(KC)], D, rstd, rk)
        for k in range(KC):
            P.add("dve", lambda e, o=At[:, k, :], a=xt[:, k, :], g=self.col(gname, gidx0 + k):
                  e.scalar_tensor_tensor(out=o, in0=a, scalar=g, in1=rstd, op0=ALU.mult, op1=ALU.mult),
                  r=[(xk, k), rk], w=[(ak, k)])

    def phase_mem_prep(self):
        A, P = self.A, self.P
        A.reset(self.base)
        self.ps_cfg = {"main": list(range(8))}
        mt = A.tile((KC, NMEM))
        rstd = A.tile(NMEM)
        self.mkpool("sq", 3, 512, BF16)
        v = self.memT.rearrange("(kc p) t -> p kc t", p=128)
        for k0 in range(0, KC, 4):
            self.load(mt[:, k0:k0 + 4, :], v[:, k0:k0 + 4, :], [("m", k) for k in range(k0, k0 + 4)])
        self.rstd_from([(mt[:, k, :], ("m", k)) for k in range(KC)], D, rstd, "mrstd", T=NMEM)
        for k in range(KC):
            P.add("dve", lambda e, o=self.memnT[:, k, :], a=mt[:, k, :], g=self.col("nmem", k):
                  e.scalar_tensor_tensor(out=o, in0=a, scalar=g, in1=rstd, op0=ALU.mult, op1=ALU.mult),
                  r=[("m", k), "mrstd"], w=[("memn", k)])
        P.barrier()

    def phase_mem_kv(self, li):
        A, P = self.A, self.P
        A.reset(self.base)
        self.ps_cfg = {"main": list(range(8))}
        w2d = self.wb["mem_w_kv"][li]
        wt = A.tile((KC, 1024), BF16)
        src = w2d.rearrange("(kc p) f -> p kc f", p=128)
        for k0 in range(0, KC, 4):
            self.P.add("sp", lambda e, d=wt[:, k0:k0 + 4, :], s_=src[:, k0:k0 + 4, :]: e.dma_start(out=d, in_=s_),
                       r=self.wkeys[id_of(w2d)], w=[("wkv", k0)], dma=True)
        for h in range(4):
            bank, bkey = self.ps()
            for k in range(KC):
                P.add("pe", lambda e, o=bank[:, 0:NMEM], l=wt[:, k, h * 128:(h + 1) * 128], rr=self.memnT[:, k, :],
                      st=(k == 0), sp=(k == KC - 1): e.matmul(o, l, rr, start=st, stop=sp),
                      r=[("wkv", k // 4 * 4)], w=[bkey])
            self.evac(self.mkT[:, h, :], bank[:, 0:NMEM], [bkey], [("mkT", h)])
        for jc in range(2):
            bank, bkey = self.ps()
            for k in range(KC):
                P.add("pe", lambda e, o=bank, l=self.memnT[:, k, jc * 128:(jc + 1) * 128], rr=wt[:, k, 512:1024],
                      st=(k == 0), sp=(k == KC - 1): e.matmul(o, l, rr, start=st, stop=sp),
                      r=[("wkv", k // 4 * 4)], w=[bkey])
            self.evac(self.mv[:, jc, :], bank, [bkey], [("mv", jc)])
        P.barrier()

    def _inproj_setup(self):
        A = self.A
        A.reset(self.base)
        self.ps_cfg = {"main": list(range(8))}
        xt1 = A.tile((KC, TT))
        self.xts = [xt1, xt1]
        self.Ats = [A.tile((KC, TT), BF16) for _ in range(2)]
        self.rstds = [A.tile(TT) for _ in range(2)]
        self.mk_wpool(3)
        self.mkpool("sq", 3, 512, BF16)
        self.mkpool("st32", 4, 512)
        self.mkpool("st16", 4, 512, BF16)

    def prep_x(self, xsrc, tt, gname, gidx0, load, norm):
        b = tt % 2
        if load:
            self.load_x(self.xts[b], xsrc, tt * TT, xk="x")
        if norm:
            self.norm_to_A(self.xts[b], self.Ats[b], gname, gidx0, self.rstds[b], xk="x", ak=("A", b), rk=("rstd", b))

    def fm_store(self, dst_rows, t0, dt, scale=None, func=None):
        P = self.P

        def consume(ci, bank, bkey, m):
            st, sk = self.rot("st32" if dt == F32 else "st16")
            if func is not None:
                P.add("act", lambda e: e.activation(out=st[0:m, :], in_=bank[0:m, :], func=func), r=[bkey], w=[sk])
            else:
                self.evac(st[0:m, :], bank[0:m, :], [bkey], [sk], scale=scale)
            self.store(dst_rows[ci * 128:ci * 128 + m, t0:t0 + TT], st[0:m, :], [sk])
        return consume

    def tm_store(self, dst, t0, c_dst0):
        def consume(tg, b0, nb, bank, bkey):
            st, sk = self.rot("st16")
            self.evac(st[:, 0:nb], bank[:, 0:nb], [bkey], [sk])
            self.store(dst[t0 + tg * 128:t0 + (tg + 1) * 128, c_dst0 + b0:c_dst0 + b0 + nb], st[:, 0:nb], [sk])
        return consume

    def phase_inproj_gla(self, li, xsrc):
        P = self.P
        self._inproj_setup()
        A = self.A
        glrT = A.tile(TT, BF16)
        etmp = A.tile(TT)
        w2d = self.wb["gla_w_in"][li]
        P.add("pool", lambda e: e.dma_start(out=self.wg[0:16, :], in_=self.wg_d[li]), w=["wg"], dma=True)
        self.prep_x(xsrc, 0, "nmix", li * 16, load=True, norm=True)
        for tt in range(NT):
            t0 = tt * TT
            b = tt % 2
            At, AK = self.Ats[b], ("A", b)
            if tt + 1 < NT:
                self.prep_x(xsrc, tt + 1, "nmix", li * 16, load=True, norm=False)
            self.gemm_fm(At, AK, KC, w2d, 0, 1024, self.fm_store(self.QT, t0, F32, scale=256 ** -0.5))
            self.gemm_fm(At, AK, KC, w2d, 1024, 1024, self.fm_store(self.KT, t0, F32))
            if tt + 1 < NT:
                self.prep_x(xsrc, tt + 1, "nmix", li * 16, load=False, norm=True)
            self.gemm_tm(At, AK, KC, w2d, 2048, 1536, self.tm_store(self.VTM, t0, 0))

            def glr_consume(ci, bank, bkey, m):
                self.evac(glrT[0:16, :], bank[0:16, :], [bkey], ["glrT"])
            self.gemm_fm(At, AK, KC, w2d, 3584, 16, glr_consume)
            for c in range(8):
                bank, bkey = self.ps()
                P.add("pe", lambda e, o=bank, l=self.wg[0:16, c * 128:(c + 1) * 128]: e.matmul(o, l, glrT[0:16, :], start=True, stop=True),
                      r=["wg", "glrT"], w=[bkey])
                nb = self.negbg[:, li * 8 + c:li * 8 + c + 1]
                P.add("act", lambda e, o=bank, nb=nb: e.activation(out=etmp, in_=o, func=AF.Exp, bias=nb, scale=-1.0),
                      r=[bkey, "negbg"], w=["etmp"])
                P.add("act", lambda e: e.activation(out=etmp, in_=etmp, func=AF.Ln, bias=1.0), r=["etmp"], w=["etmp"])
                st, sk = self.rot("st32")
                P.add("dve", lambda e, st=st: e.tensor_scalar(out=st, in0=etmp, scalar1=-1.0 / 16.0, scalar2=None, op0=ALU.mult),
                      r=["etmp"], w=[sk])
                self.store(self.GT[c * 128:(c + 1) * 128, t0:t0 + TT], st, [sk])
            self.gemm_fm(At, AK, KC, w2d, 3600, 1536, self.fm_store(self.OGT, t0, F32, func=AF.Silu))
            self.gemm_fm(At, AK, KC, w2d, 5136, 512, self.fm_store(self.MQT, t0, BF16, scale=128 ** -0.5))
        P.barrier()

    def phase_inproj_fox(self, j, li, xsrc):
        P = self.P
        self._inproj_setup()
        w2d = self.wb["fox_w_in"][j]
        self.prep_x(xsrc, 0, "nmix", li * 16, load=True, norm=True)
        for tt in range(NT):
            t0 = tt * TT
            b = tt % 2
            At, AK = self.Ats[b], ("A", b)
            if tt + 1 < NT:
                self.prep_x(xsrc, tt + 1, "nmix", li * 16, load=True, norm=False)
            self.gemm_fm(At, AK, KC, w2d, 0, 512, self.fm_store(self.FQT, t0, BF16, scale=128 ** -0.5))
            if tt + 1 < NT:
                self.prep_x(xsrc, tt + 1, "nmix", li * 16, load=False, norm=True)
            self.gemm_fm(At, AK, KC, w2d, 512, 1024, self.fm_store(self.FQT[512:1536], t0, BF16, scale=128 ** -0.5))
            self.gemm_fm(At, AK, KC, w2d, 1536, 512, self.fm_store(self.MQT, t0, BF16, scale=128 ** -0.5))
        P.barrier()

    def finish(self):
        from contextlib import ExitStack
        for n, dst in self.debug_outs.items():
            src = self.scratch[n]
            if len(src.shape) == 3:
                for i in range(src.shape[0]):
                    self.P.add("sp", lambda e, d=dst[i], s=src[i]: e.dma_start(out=d, in_=s), dma=True)
            else:
                rows = src.shape[0]
                step = max(1, rows // 8)
                for r0 in range(0, rows, step):
                    self.P.add("sp", lambda e, d=dst[r0:r0 + step], s=src[r0:r0 + step]: e.dma_start(out=d, in_=s), dma=True)
        self.P.barrier()
        with ExitStack() as st:
            self.P.emit(st)
        return self.nc


def make_in_maps(inputs, cores=8):
    cols = pack_cols(inputs)
    consts = make_consts()
    shared = {"cols": cols, "consts": consts,
              "gla_w_gate_up": np.ascontiguousarray(inputs["gla_w_gate_up"], dtype=np.float32)}
    for n, _ in WEIGHTS:
        shared[n] = np.ascontiguousarray(inputs[n], dtype=np.float32)
    maps = []
    for b in range(cores):
        m = dict(shared)
        m["xT"] = np.ascontiguousarray(np.asarray(inputs["x"][b], np.float32).T)
        m["memT"] = np.ascontiguousarray(np.asarray(inputs["mem"][b], np.float32).T)
        maps.append(m)
    return maps


def _attn_methods():
    pass


class Builder2(Builder):
    def attn_tile(self, items, qT, qkey, t0q, out_rows, t0, augq=None):
        P = self.P
        pv, pvk = self.ps("pv")
        dn, dnk = self.ps("dn")
        n = len(items)
        pts = [None] * n

        def qk(i):
            it = items[i]
            o = it["o"]
            sb, sk = self.ps("s")
            it["sb"], it["sk"] = sb, sk
            diag = o is not None
            o = o or 0
            aug = it.get("aug")
            P.add("pe", lambda e, out=sb[:, o:TT], l=it["kT"], rr=qT[:, t0q + o:t0q + TT], sp=(aug is None and not diag):
                  e.matmul(out, l, rr, start=True, stop=sp), r=it["kkeys"] + [qkey], w=[sk])
            if aug is not None:
                P.add("pe", lambda e, out=sb[:, o:TT], l=aug, rr=augq[0:6, t0q + o:t0q + TT], sp=(not diag):
                      e.matmul(out, l, rr, start=False, stop=sp), r=it["kkeys"] + [qkey], w=[sk])
            if diag:
                P.add("pe", lambda e, out=sb[:, o:o + 128]: e.matmul(out, self.identb, self.negb, start=False, stop=True),
                      r=[], w=[sk])
            pt, ptk = self.rot("pt")
            pts[i] = (pt, ptk)
            P.add("act", lambda e, out=pt[:, o:TT], a=sb[:, o:TT]: e.activation(out=out, in_=a, func=AF.Exp), r=[sk], w=[ptk])

        def pvd(i):
            it = items[i]
            o = it["o"] or 0
            pt, ptk = pts[i]
            P.add("pe", lambda e, out=pv[:, o:TT], l=it["v"], rr=pt[:, o:TT], st=(i == 0), sp=(i == n - 1):
                  e.matmul(out, l, rr, start=st, stop=sp), r=it["kkeys"] + [ptk], w=[pvk])
            P.add("pe", lambda e, out=dn[:, o:TT], rr=pt[:, o:TT], st=(i == 0), sp=(i == n - 1):
                  e.matmul(out, self.onesb, rr, start=st, stop=sp), r=[ptk], w=[dnk])
        LAG = 2
        for i in range(n + LAG):
            if i < n:
                qk(i)
            if i >= LAG:
                pvd(i - LAG)
        rc, rck = self.rot("rc")
        P.add("dve", lambda e: e.reciprocal(out=rc, in_=dn), r=[dnk], w=[rck])
        st, sk = self.rot("st16")
        P.add("dve", lambda e: e.tensor_tensor(out=st, in0=pv, in1=rc, op=ALU.mult), r=[pvk, rck], w=[sk])
        self.store(self.MIXT[out_rows:out_rows + 128, t0:t0 + TT], st, [sk])

    def _attn_setup(self):
        A = self.A
        A.reset(self.base)
        self.ps_cfg = {"main": list(range(8)), "pv": [0, 1], "dn": [2, 3], "s": [4, 5, 6, 7]}
        self.mkpool("pt", 4, 512, BF16)
        self.mkpool("rc", 2, 512)
        self.mkpool("st16", 3, 512, BF16)

    def phase_mem_attn(self):
        self._attn_setup()
        A, P = self.A, self.P
        self.mkpool("mq", 2, S, BF16)
        for h in range(4):
            qT, qk = self.rot("mq")
            self.load(qT, self.MQT[h * 128:(h + 1) * 128, :], [qk])
            items = [dict(kT=self.mkT[:, h, jc * 128:(jc + 1) * 128], kkeys=[], v=self.mv[:, jc, h * 128:(h + 1) * 128], o=None)
                     for jc in range(2)]
            for tt in range(NT):
                its = [dict(it) for it in items]
                self.attn_tile(its, qT, qk, tt * TT, 1536 + h * 128, tt * TT)
        P.barrier()

    def phase_fox_attn(self):
        self._attn_setup()
        A, P = self.A, self.P
        self.mkpool("fq", 2, S, BF16)
        self.mkpool("fk", 2, S, BF16)
        self.mkpool("fv", 2, (32, 128), BF16)
        self.mkpool("aq", 2, S, BF16)
        self.mkpool("ak", 2, S, BF16)
        for h in range(12):
            qT, qk = self.rot("fq")
            kT, kk = self.rot("fk")
            vt, vk = self.rot("fv")
            aq, aqk = self.rot("aq")
            ak, akk = self.rot("ak")
            self.load(qT, self.FQT[h * 128:(h + 1) * 128, :], [qk])
            self.load(kT, self.FKT[h * 128:(h + 1) * 128, :], [kk])
            vsrc = self.FVTM[:, h * 128:(h + 1) * 128].rearrange("(c p) d -> p c d", p=128)
            for c0 in range(0, 32, 8):
                self.load(vt[:, c0:c0 + 8, :], vsrc[:, c0:c0 + 8, :], [(vk, c0)])
            self.load(aq[0:6, :], self.AUGQ[h], [aqk])
            self.load(ak[0:6, :], self.AUGK[h], [akk])
            for tt in range(NT):
                items = []
                for i in range(4 * tt + 4):
                    o = None if i < 4 * tt else (i - 4 * tt) * 128
                    items.append(dict(kT=kT[:, i * 128:(i + 1) * 128], kkeys=[kk, (vk, i // 8 * 8), akk, aqk],
                                      v=vt[:, i, :], aug=ak[0:6, i * 128:(i + 1) * 128], o=o))
                self.attn_tile(items, qT, qk, tt * TT, h * 128, tt * TT, augq=aq)
        P.barrier()

    def phase_gla(self, li):
        A, P = self.A, self.P
        C = GC
        NCH = TT // C
        A.reset(self.base)
        self.ps_cfg = {"main": [0], "tr": [1, 2], "att": [3], "o": [4, 5], "st": [6, 7]}
        S32 = A.tile((4, 2, 384))
        Sb = A.tile((4, 2, 384), BF16)
        Obuf = A.tile((4, 3, TT))
        og = A.tile((12, TT))
        qtl = [A.tile((2, TT), BF16) for _ in range(4)]
        ktl = [A.tile((2, TT), BF16) for _ in range(4)]
        khl = [A.tile((2, TT), BF16) for _ in range(4)]
        ell = [A.tile((2, NCH)) for _ in range(4)]
        vcl = [A.tile((NCH, 384), BF16) for _ in range(4)]
        self.mkpool("q32", 2, (2, TT))
        self.mkpool("k32", 2, (2, TT))
        self.mkpool("g32", 2, (2, TT))
        self.mkpool("E", 2, (2, TT))
        self.mkpool("Ei", 2, (2, TT))
        self.mkpool("khT", 3, 256, BF16)
        self.mkpool("attb", 3, C, BF16)
        self.mkpool("sq", 3, 512, BF16)
        self.mkpool("rstd", 2, 512)
        self.mkpool("y", 2, 512)
        self.mkpool("st16", 3, 512, BF16)
        P.add("pool", lambda e: e.memset(S32, 0.0), w=[("S32", h, k) for h in range(4) for k in range(2)])
        P.add("pool", lambda e: e.memset(Sb, 0.0), w=[("Sb", h, k) for h in range(4) for k in range(2)])

        def stage_a(c, h):
            cs = slice(c * C, (c + 1) * C)
            trb, trk = self.ps("tr")
            trv = trb.bitcast(BF16)
            for kc in range(2):
                P.add("pe", lambda e, o=trv[0:C, kc * 128:(kc + 1) * 128], a=khl[h][:, kc, cs]: e.transpose(o, a, self.identb),
                      r=[("kh", h, kc)], w=[trk])
            khT, khTk = self.rot("khT")
            P.add("act", lambda e, o=khT[0:C, :], a=trv[0:C, 0:256]: e.activation(out=o, in_=a, func=AF.Copy), r=[trk], w=[khTk])
            ab, abk = self.ps("att")
            for kc in range(2):
                P.add("pe", lambda e, o=ab[0:C, 0:C], l=ktl[h][:, kc, cs], rr=qtl[h][:, kc, cs], st=(kc == 0), sp=(kc == 1):
                      e.matmul(o, l, rr, start=st, stop=sp), r=[("kt", h, kc), ("qt", h, kc)], w=[abk])
            attb, attk = self.rot("attb")
            P.add("dve", lambda e, o=attb[0:C, :], a=ab[0:C, 0:C]: e.tensor_tensor(out=o, in0=a, in1=self.trib[0:C, 0:C], op=ALU.mult),
                  r=[abk], w=[attk])
            return khT, khTk, attb, attk

        def stage_b(c, h, khT, khTk, attb, attk):
            cs = slice(c * C, (c + 1) * C)
            ob, obk = self.ps("o")
            for vc in range(3):
                for kc in range(2):
                    P.add("pe", lambda e, o=ob[:, vc * C:(vc + 1) * C], l=Sb[:, h, kc, vc * 128:(vc + 1) * 128], rr=qtl[h][:, kc, cs],
                          st=(kc == 0): e.matmul(o, l, rr, start=st, stop=False),
                          r=[("Sb", h, kc), ("qt", h, kc)], w=[obk])
                P.add("pe", lambda e, o=ob[:, vc * C:(vc + 1) * C], l=vcl[h][0:C, c, vc * 128:(vc + 1) * 128], rr=attb[0:C, :]:
                      e.matmul(o, l, rr, start=False, stop=True), r=[("vc", h), attk], w=[obk])
            P.add("act", lambda e, o=Obuf[:, h, :, cs], a=ob[:, 0:3 * C].rearrange("p (v t) -> p v t", t=C):
                  e.activation(out=o, in_=a, func=AF.Copy), r=[obk], w=[("O", h, c)])
            for kc in range(2):
                sb_, sbk = self.ps("st")
                P.add("pe", lambda e, o=sb_[:, 0:384], l=khT[0:C, kc * 128:(kc + 1) * 128], rr=vcl[h][0:C, c, :]:
                      e.matmul(o, l, rr, start=True, stop=True), r=[khTk, ("vc", h)], w=[sbk])
                P.add("dve", lambda e, o=S32[:, h, kc, :], sc=ell[h][:, kc, c:c + 1], a=sb_[:, 0:384]:
                      e.scalar_tensor_tensor(out=o, in0=o, scalar=sc, in1=a, op0=ALU.mult, op1=ALU.add),
                      r=[sbk, ("el", h, kc), ("S32", h, kc)], w=[("S32", h, kc)])
                P.add("act", lambda e, o=Sb[:, h, kc, :], a=S32[:, h, kc, :]: e.activation(out=o, in_=a, func=AF.Copy),
                      r=[("S32", h, kc)], w=[("Sb", h, kc)])

        for tt in range(NT):
            t0 = tt * TT
            self.pump(4)
            for h in range(4):
                q32, qk_ = self.rot("q32")
                k32, kk_ = self.rot("k32")
                g32, gk_ = self.rot("g32")
                E, Ek = self.rot("E")
                Ei, Eik = self.rot("Ei")
                rows = slice(h * 256, (h + 1) * 256)
                self.load(q32, self.QT[rows, t0:t0 + TT].rearrange("(kc p) t -> p kc t", p=128), [qk_])
                self.load(k32, self.KT[rows, t0:t0 + TT].rearrange("(kc p) t -> p kc t", p=128), [(kk_, 0), (kk_, 1)])
                self.load(g32, self.GT[rows, t0:t0 + TT].rearrange("(kc p) t -> p kc t", p=128), [gk_])
                self.load(vcl[h][0:C, :, :], self.VTM[t0:t0 + TT, h * 384:(h + 1) * 384].rearrange("(c p) f -> p c f", p=C),
                          [("vc", h)])
                for kc in range(2):
                    P.add("dve", lambda e, b=g32[:, kc, :]: e.tensor_tensor_scan(out=b, data0=self.scanmask, data1=b, initial=0.0,
                                                                              op0=ALU.mult, op1=ALU.add), r=[gk_], w=[gk_])
                    P.add("act", lambda e, o=E[:, kc, :], b=g32[:, kc, :]: e.activation(out=o, in_=b, func=AF.Exp), r=[gk_], w=[(Ek, kc)])
                    P.add("act", lambda e, o=Ei[:, kc, :], b=g32[:, kc, :]: e.activation(out=o, in_=b, func=AF.Exp, scale=-1.0),
                          r=[gk_], w=[(Eik, kc)])
                    P.add("dve", lambda e, o=qtl[h][:, kc, :], a=q32[:, kc, :], b=E[:, kc, :]: e.tensor_tensor(out=o, in0=a, in1=b, op=ALU.mult),
                          r=[qk_, (Ek, kc)], w=[("qt", h, kc)])
                    P.add("dve", lambda e, o=k32[:, kc, :], b=Ei[:, kc, :]: e.tensor_tensor(out=o, in0=o, in1=b, op=ALU.mult),
                          r=[(Eik, kc), (kk_, kc)], w=[(kk_, kc)])
                    P.add("act", lambda e, o=ktl[h][:, kc, :], a=k32[:, kc, :]: e.activation(out=o, in_=a, func=AF.Copy),
                          r=[(kk_, kc)], w=[("kt", h, kc)])
                    elv = E[:, kc, :].rearrange("p (c t) -> p c t", t=C)[:, :, C - 1]
                    P.add("pool", lambda e, o=ell[h][:, kc, :], a=elv: e.tensor_copy(out=o, in_=a), r=[(Ek, kc)], w=[("el", h, kc)])
                    elb = E[:, kc, :].rearrange("p (c t) -> p c t", t=C)[:, :, C - 1:C].to_broadcast([128, NCH, C])
                    P.add("pool", lambda e, o=khl[h][:, kc, :].rearrange("p (c t) -> p c t", t=C),
                          a=k32[:, kc, :].rearrange("p (c t) -> p c t", t=C), b=elb: e.tensor_tensor(out=o, in0=a, in1=b, op=ALU.mult),
                          r=[(kk_, kc), (Ek, kc)], w=[("kh", h, kc)])
            items = [(c, h) for c in range(NCH) for h in range(4)]
            pend = None
            for it in items:
                a_out = stage_a(*it)
                if pend is not None:
                    stage_b(*pend)
                pend = it + a_out
            stage_b(*pend)
            ogv = self.OGT[:, t0:t0 + TT].rearrange("(c p) t -> p c t", p=128)
            for c0 in range(0, 12, 4):
                self.load(og[:, c0:c0 + 4, :], ogv[:, c0:c0 + 4, :], [("og", c0)])
            for h in range(4):
                rstd, rk = self.rot("rstd")
                okeys = [("O", h, c) for c in range(NCH)]
                self.rstd_from([(Obuf[:, h, vc, :], okeys) for vc in range(3)], 384, rstd, rk)
                for vc in range(3):
                    y, yk = self.rot("y")
                    ci = h * 3 + vc
                    P.add("dve", lambda e, o=y, a=Obuf[:, h, vc, :], g=self.col("gnorm", li * 12 + ci), rs=rstd:
                          e.scalar_tensor_tensor(out=o, in0=a, scalar=g, in1=rs, op0=ALU.mult, op1=ALU.mult),
                          r=okeys + [rk], w=[yk])
                    st, sk = self.rot("st16")
                    P.add("pool", lambda e, o=st, a=y, b=og[:, ci, :]: e.tensor_tensor(out=o, in0=a, in1=b, op=ALU.mult),
                          r=[yk, ("og", ci // 4 * 4)], w=[sk])
                    self.store(self.MIXT[ci * 128:(ci + 1) * 128, t0:t0 + TT], st, [sk])
        P.barrier()

    def phase_out_ffn(self, li, wout2d, xsrc, final=False):
        A, P = self.A, self.P
        A.reset(self.base)
        self.ps_cfg = {"main": list(range(8))}
        xt = A.tile((KC, TT))
        At = A.tile((KC, TT), BF16)
        hid = A.tile((44, TT), BF16)
        rstd = A.tile(TT)
        halo = A.tile((88, 2))
        self.mk_wpool(6, 256)
        self.mkpool("sq", 3, 512, BF16)
        self.mkpool("Ua", 2, 514)
        self.mkpool("Uv", 2, 514)
        self.mkpool("acca", 2, 512)
        self.mkpool("accv", 2, 512)
        self.mkpool("sil", 2, 512)
        self.mkpool("st32", 3, 512)
        wup = self.wb["ffn_w_up"][li]
        wdn = self.wb["ffn_w_down"][li]
        P.add("pool", lambda e: e.memset(halo, 0.0), w=[("halo", c) for c in range(88)])
        cw = COLS["convw"] + li * 3 * 88
        cb = COLS["convb"] + li * 88

        def resid(ci, bank, bkey, m):
            P.add("dve", lambda e, o=xt[:, ci, :], b=bank: e.tensor_tensor(out=o, in0=o, in1=b, op=ALU.add),
                  r=[bkey, ("x", ci)], w=[("x", ci)])

        def conv(bank, bkey, cc, nm):
            U, Uk = self.rot("U" + nm)
            acc, ak = self.rot("acc" + nm)
            P.add("pool", lambda e, o=U[:, 0:2], a=halo[:, cc, :]: e.tensor_copy(out=o, in_=a), r=[("halo", cc)], w=[(Uk, "h")])
            P.add("act", lambda e, o=U[:, 2:514], a=bank: e.activation(out=o, in_=a, func=AF.Copy), r=[bkey], w=[(Uk, "b")])
            P.add("pool", lambda e, o=halo[:, cc, :], a=U[:, 512:514]: e.tensor_copy(out=o, in_=a), r=[(Uk, "b")], w=[("halo", cc)])
            w0 = self.cols[:, cw + 0 * 88 + cc:cw + 0 * 88 + cc + 1]
            w1 = self.cols[:, cw + 1 * 88 + cc:cw + 1 * 88 + cc + 1]
            w2 = self.cols[:, cw + 2 * 88 + cc:cw + 2 * 88 + cc + 1]
            bb = self.cols[:, cb + cc:cb + cc + 1]
            P.add("dve", lambda e, o=acc, a=U[:, 2:514]: e.tensor_scalar(out=o, in0=a, scalar1=w2, scalar2=bb, op0=ALU.mult, op1=ALU.add),
                  r=[(Uk, "b")], w=[ak])
            P.add("dve", lambda e, o=acc, a=U[:, 1:513]: e.scalar_tensor_tensor(out=o, in0=a, scalar=w1, in1=o, op0=ALU.mult, op1=ALU.add),
                  r=[(Uk, "b"), (Uk, "h"), ak], w=[ak])
            P.add("dve", lambda e, o=acc, a=U[:, 0:512]: e.scalar_tensor_tensor(out=o, in0=a, scalar=w0, in1=o, op0=ALU.mult, op1=ALU.add),
                  r=[(Uk, "b"), (Uk, "h"), ak], w=[ak])
            return acc, ak

        for tt in range(NT):
            t0 = tt * TT
            self.load_x(xt, xsrc, t0)
            mv_ = self.MIXT[:, t0:t0 + TT].rearrange("(kc p) t -> p kc t", p=128)
            for k0 in range(0, KC, 4):
                self.load(At[:, k0:k0 + 4, :], mv_[:, k0:k0 + 4, :], [("A", k) for k in range(k0, k0 + 4)])
            self.gemm_fm(At, "A", KC, wout2d, 0, D, resid, bw=256)
            self.norm_to_A(xt, At, "nffn", li * 16, rstd)
            for hb in range(22):
                wa, wak = self.wload(wup, 0, KC, hb * 256, 256)
                wv, wvk = self.wload(wup, 0, KC, FFN_H + hb * 256, 256)
                for ch in range(2):
                    c = hb * 2 + ch
                    ba, bak = self.ps()
                    for k in range(KC):
                        P.add("pe", lambda e, o=ba, l=wa[:, k, ch * 128:(ch + 1) * 128], rr=At[:, k, :], st=(k == 0), sp=(k == KC - 1):
                              e.matmul(o, l, rr, start=st, stop=sp), r=[wak, ("A", k)], w=[bak])
                    bv, bvk = self.ps()
                    for k in range(KC):
                        P.add("pe", lambda e, o=bv, l=wv[:, k, ch * 128:(ch + 1) * 128], rr=At[:, k, :], st=(k == 0), sp=(k == KC - 1):
                              e.matmul(o, l, rr, start=st, stop=sp), r=[wvk, ("A", k)], w=[bvk])
                    acca, aak = conv(ba, bak, c, "a")
                    accv, avk = conv(bv, bvk, 44 + c, "v")
                    sil, sk = self.rot("sil")
                    P.add("act", lambda e, o=sil, a=acca: e.activation(out=o, in_=a, func=AF.Silu), r=[aak], w=[sk])
                    P.add("pool", lambda e, o=hid[:, c, :], a=sil, b=accv: e.tensor_tensor(out=o, in0=a, in1=b, op=ALU.mult),
                          r=[sk, avk], w=[("hid", c)])
            self.gemm_fm(hid, "hid", 44, wdn, 0, D, resid, kblock=16, bw=256)
            if not final:
                self.store_x(xt, self.XT, t0)
            else:
                self.rstd_from([(xt[:, k, :], ("x", k)) for k in range(KC)], D, rstd, "rstd")
                ov = self.outT[:, t0:t0 + TT].rearrange("(kc p) t -> p kc t", p=128)
                for k in range(KC):
                    st, sk = self.rot("st32")
                    P.add("dve", lambda e, o=st, a=xt[:, k, :], g=self.col("nfin", k):
                          e.scalar_tensor_tensor(out=o, in0=a, scalar=g, in1=rstd, op0=ALU.mult, op1=ALU.mult),
                          r=[("x", k), "rstd"], w=[sk])
                    self.store(ov[:, k, :], st, [sk])
        P.barrier()

    def phase_fox_kv(self, xsrc):
        P = self.P
        self._inproj_setup()
        A = self.A
        lf = A.tile(S)
        cb1 = A.tile(S, BF16)
        cb2 = A.tile(S, BF16)
        etmp = A.tile(TT)
        onesS = cb2
        w2d = self.wb["fox_w_kv"]
        self.prep_x(xsrc, 0, "nfkv", 0, load=True, norm=True)
        for tt in range(NT):
            t0 = tt * TT
            b = tt % 2
            At, AK = self.Ats[b], ("A", b)
            if tt + 1 < NT:
                self.prep_x(xsrc, tt + 1, "nfkv", 0, load=True, norm=False)
            self.gemm_fm(At, AK, KC, w2d, 0, 1536, self.fm_store(self.FKT, t0, BF16))
            if tt + 1 < NT:
                self.prep_x(xsrc, tt + 1, "nfkv", 0, load=False, norm=True)
            self.gemm_tm(At, AK, KC, w2d, 1536, 1536, self.tm_store(self.FVTM, t0, 0))

            def fl_consume(ci, bank, bkey, m, t0=t0):
                P.add("act", lambda e: e.activation(out=etmp[0:12, :], in_=bank[0:12, :], func=AF.Exp, bias=self.negbf[0:12, :], scale=-1.0),
                      r=[bkey], w=["etmp"])
                P.add("act", lambda e: e.activation(out=etmp[0:12, :], in_=etmp[0:12, :], func=AF.Ln, bias=1.0), r=["etmp"], w=["etmp"])
                P.add("dve", lambda e: e.tensor_scalar(out=lf[0:12, t0:t0 + TT], in0=etmp[0:12, :], scalar1=-1.0, scalar2=None, op0=ALU.mult),
                      r=["etmp"], w=["lf"])
            self.gemm_fm(At, AK, KC, w2d, 3072, 12, fl_consume)
        for tt in range(NT):
            t0 = tt * TT
            init = 0.0 if tt == 0 else lf[0:12, t0 - 1:t0]
            P.add("dve", lambda e, o=lf[0:12, t0:t0 + TT], init=init: e.tensor_tensor_scan(out=o, data0=self.ones512[0:12, :], data1=o, initial=init,
                                                                                       op0=ALU.mult, op1=ALU.add), r=["lf"], w=["lf"])
        P.add("pool", lambda e: e.memset(onesS[0:12, :], 1.0), w=["cb2"])
        for r_ in range(3):
            self.store(self.AUGQ[:, 3 + r_, :], onesS[0:12, :], ["cb2"])
            self.store(self.AUGK[:, r_, :], onesS[0:12, :], ["cb2"])
        for r_ in range(3):
            P.add("dve", lambda e: e.tensor_copy(out=cb1[0:12, :], in_=lf[0:12, :]), r=["lf"], w=["cb1"])
            self.store(self.AUGQ[:, r_, :], cb1[0:12, :], ["cb1"])
            if r_ < 2:
                P.add("dve", lambda e: e.tensor_tensor(out=lf[0:12, :], in0=lf[0:12, :], in1=cb1[0:12, :], op=ALU.subtract), r=["lf", "cb1"], w=["lf"])
            P.add("dve", lambda e: e.tensor_scalar(out=cb2[0:12, :], in0=cb1[0:12, :], scalar1=-1.0, scalar2=None, op0=ALU.mult),
                  r=["cb1"], w=["cb2"])
            self.store(self.AUGK[:, 3 + r_, :], cb2[0:12, :], ["cb2"])
        P.barrier()


def build_full(upto=99, debug_outs=()):
    B = Builder2(debug_outs=debug_outs)
    B.ensure("gla_w_in", 0)
    B.phase_mem_prep()
    step = 0
    xsrc = B.xT_in
    for li in range(4):
        B.phase_mem_kv(li)
        if li < 2:
            B.ensure("gla_w_in", li)
            B.phase_inproj_gla(li, xsrc)
            step += 1
            if step >= upto:
                break
            B.phase_gla(li)
            wout = B.wb["gla_w_out"][li]
        else:
            if li == 2:
                B.ensure("fox_w_kv", None)
                B.phase_fox_kv(xsrc)
            B.ensure("fox_w_in", li - 2)
            B.phase_inproj_fox(li - 2, li, xsrc)
            step += 1
            if step >= upto:
                break
            B.phase_fox_attn()
            wout = B.wb["fox_w_out"][li - 2]
        B.phase_mem_attn()
        step += 1
        if step >= upto:
            break
        B.ensure("ffn_w_down", li)
        B.phase_out_ffn(li, wout, xsrc, final=(li == 3))
        xsrc = B.XT
        step += 1
        if step >= upto:
            break
    return B, B.finish()


_CACHE = {}


def kernel(**inputs):
    if "nc" not in _CACHE:
        _CACHE["nc"] = build_full()[1]
    nc = _CACHE["nc"]
    maps = make_in_maps(inputs, cores=8)
    res = run_bass_kernel_spmd(nc, maps, core_ids=list(range(8)))
    out = np.stack([np.ascontiguousarray(np.asarray(r["outT"], np.float32).T) for r in res.results], axis=0)
    return out
```
